# Optimizing a Trainium2 kernel written in Bass

```python
import math
import jax, jax.numpy as jnp
from jax import lax
import numpy as np

D_MODEL = 1024
BATCH = 32
SEQ = 2048
DEPTH = 2

D_FF = 2816
FFN_RESIDUAL_SCALE = 0.5
EPS = 1e-6
BLOCK_Q = 128

A_WIDTH = 512
A_CONV = 3
B_WIDTH = 512
B_CONV = 31
C_HEADS = 4
C_QK_DIM = 64
C_V_DIM = 2 * C_QK_DIM
C_QK_WIDTH = C_HEADS * 2 * C_QK_DIM
C_V_WIDTH = C_HEADS * C_V_DIM
D_HEADS = 8
D_HEAD_DIM = 64
D_WIDTH = D_HEADS * D_HEAD_DIM
N_BRANCHES = 4

IN_SPLITS = (A_WIDTH, A_WIDTH, A_WIDTH,
             2 * B_WIDTH,
             C_QK_WIDTH, C_QK_WIDTH, C_V_WIDTH,
             D_WIDTH, D_WIDTH, D_WIDTH, D_HEADS,
             N_BRANCHES * D_MODEL)
IN_WIDTH = sum(IN_SPLITS)

kernel_name = 'hybrid_parallel_mixer_block'


def _split(x, sizes):
    idx = []
    acc = 0
    for s in sizes[:-1]:
        acc += s
        idx.append(acc)
    return jnp.split(x, idx, axis=-1)


def rms_norm(x, g):
    xf = x.astype(jnp.float32)
    y = xf * lax.rsqrt(jnp.mean(xf * xf, axis=-1, keepdims=True) + EPS)
    return (y * g.astype(jnp.float32)).astype(x.dtype)


def layer_norm(x, g, b):
    xf = x.astype(jnp.float32)
    mu = jnp.mean(xf, axis=-1, keepdims=True)
    xc = xf - mu
    var = jnp.mean(xc * xc, axis=-1, keepdims=True)
    return (xc * lax.rsqrt(var + EPS) * g.astype(jnp.float32) + b.astype(jnp.float32)).astype(x.dtype)


def swiglu(h, w_gate, w_up, w_down):
    return (jax.nn.silu(h @ w_gate) * (h @ w_up)) @ w_down


def causal_depthwise_conv(x, w):
    k_width, ch = w.shape
    return lax.conv_general_dilated(x, w[:, None, :], window_strides=(1,), padding=[(k_width - 1, 0)],
                                    dimension_numbers=('NWC', 'WIO', 'NWC'), feature_group_count=ch)


def short_conv_mixer(gate_b, gate_c, xin, conv_w, w_out):
    return (gate_b * causal_depthwise_conv(gate_c * xin, conv_w)) @ w_out


def conformer_conv_mixer(u, conv_w, conv_b, ln_g, ln_b, w_out):
    a, g = jnp.split(u, 2, axis=-1)
    y = a * jax.nn.sigmoid(g)
    y = causal_depthwise_conv(y, conv_w) + conv_b
    y = jax.nn.silu(layer_norm(y, ln_g, ln_b))
    return y @ w_out


def alibi_slopes(n_heads):
    return jnp.exp2(-8.0 / n_heads * jnp.arange(1, n_heads + 1, dtype=jnp.float32))


def diff_attention_mixer(q, k, v, lam_q1, lam_k1, lam_q2, lam_k2, subln_g, w_out, lambda_init):
    b, s, _ = q.shape
    q = q.reshape(b, s, C_HEADS, 2, C_QK_DIM)
    k = k.reshape(b, s, C_HEADS, 2, C_QK_DIM)
    v = v.reshape(b, s, C_HEADS, C_V_DIM)
    f32 = jnp.float32
    lam = (jnp.exp(jnp.sum(lam_q1.astype(f32) * lam_k1.astype(f32)))
           - jnp.exp(jnp.sum(lam_q2.astype(f32) * lam_k2.astype(f32))) + lambda_init)
    slopes = alibi_slopes(C_HEADS)[:, None, None]
    scale = C_QK_DIM ** -0.5
    outs = []
    for i in range(s // BLOCK_Q):
        q0, q1 = i * BLOCK_Q, (i + 1) * BLOCK_Q
        dist = (jnp.arange(q0, q1)[:, None] - jnp.arange(q1)[None, :]).astype(f32)
        bias = jnp.where(dist >= 0, -slopes * dist, -jnp.inf)
        logits = jnp.einsum('bqhmd,bkhmd->bmhqk', q[:, q0:q1], k[:, :q1]).astype(f32) * scale + bias
        p = jax.nn.softmax(logits, axis=-1)
        pd = p[:, 0] - lam * p[:, 1]
        outs.append(jnp.einsum('bhqk,bkhd->bqhd', pd.astype(v.dtype), v[:, :q1]))
    o = jnp.concatenate(outs, axis=1)
    o = rms_norm(o, subln_g) * (1.0 - lambda_init)
    return o.reshape(b, s, C_V_WIDTH) @ w_out


def forgetting_attention_mixer(q, k, v, f_logit, f_bias, w_out):
    b, s, _ = q.shape
    q = q.reshape(b, s, D_HEADS, D_HEAD_DIM)
    k = k.reshape(b, s, D_HEADS, D_HEAD_DIM)
    v = v.reshape(b, s, D_HEADS, D_HEAD_DIM)
    f32 = jnp.float32
    log_f = jax.nn.log_sigmoid(f_logit.astype(f32) + f_bias.astype(f32))
    cum = jnp.cumsum(log_f, axis=1).transpose(0, 2, 1)
    scale = D_HEAD_DIM ** -0.5
    outs = []
    for i in range(s // BLOCK_Q):
        q0, q1 = i * BLOCK_Q, (i + 1) * BLOCK_Q
        causal = jnp.arange(q0, q1)[:, None] >= jnp.arange(q1)[None, :]
        decay = cum[:, :, q0:q1, None] - cum[:, :, None, :q1]
        bias = jnp.where(causal, decay, -jnp.inf)
        logits = jnp.einsum('bqhd,bkhd->bhqk', q[:, q0:q1], k[:, :q1]).astype(f32) * scale + bias
        p = jax.nn.softmax(logits, axis=-1)
        outs.append(jnp.einsum('bhqk,bkhd->bqhd', p.astype(v.dtype), v[:, :q1]))
    o = jnp.concatenate(outs, axis=1)
    return o.reshape(b, s, D_WIDTH) @ w_out


def setup_inputs(seed: int = 0) -> dict:
    key = jax.random.key(seed)
    ks = jax.random.split(key, 40)
    it = iter(range(40))

    def nk():
        return ks[next(it)]

    def w(shape, fan_in):
        return jax.random.normal(nk(), shape, jnp.float32) * (fan_in ** -0.5)

    def gain(shape):
        return 1.0 + 0.05 * jax.random.normal(nk(), shape, jnp.float32)

    def small(shape, scale=0.02, offset=0.0):
        return offset + scale * jax.random.normal(nk(), shape, jnp.float32)

    L = DEPTH
    return {
        'x': jax.random.normal(nk(), (BATCH, SEQ, D_MODEL), jnp.float32),
        'ffn1_pre_g': gain((L, D_MODEL)),
        'ffn1_post_g': gain((L, D_MODEL)),
        'ffn1_w_gate': w((L, D_MODEL, D_FF), D_MODEL),
        'ffn1_w_up': w((L, D_MODEL, D_FF), D_MODEL),
        'ffn1_w_down': w((L, D_FF, D_MODEL), D_FF),
        'mix_pre_g': gain((L, D_MODEL)),
        'mix_post_g': gain((L, D_MODEL)),
        'w_in': w((L, D_MODEL, IN_WIDTH), D_MODEL),
        'a_conv_w': w((L, A_CONV, A_WIDTH), A_CONV),
        'a_w_out': w((L, A_WIDTH, D_MODEL), A_WIDTH),
        'b_conv_w': w((L, B_CONV, B_WIDTH), B_CONV),
        'b_conv_b': small((L, B_WIDTH)),
        'b_ln_g': gain((L, B_WIDTH)),
        'b_ln_b': small((L, B_WIDTH)),
        'b_w_out': w((L, B_WIDTH, D_MODEL), B_WIDTH),
        'c_lam_q1': small((L, C_QK_DIM), 0.1),
        'c_lam_k1': small((L, C_QK_DIM), 0.1),
        'c_lam_q2': small((L, C_QK_DIM), 0.1),
        'c_lam_k2': small((L, C_QK_DIM), 0.1),
        'c_subln_g': gain((L, C_V_DIM)),
        'c_w_out': w((L, C_V_WIDTH, D_MODEL), C_V_WIDTH),
        'd_forget_b': small((L, D_HEADS), 0.5, 3.0),
        'd_w_out': w((L, D_WIDTH, D_MODEL), D_WIDTH),
        'w_o': w((L, D_MODEL, D_MODEL), D_MODEL),
        'ffn2_pre_g': gain((L, D_MODEL)),
        'ffn2_post_g': gain((L, D_MODEL)),
        'ffn2_w_gate': w((L, D_MODEL, D_FF), D_MODEL),
        'ffn2_w_up': w((L, D_MODEL, D_FF), D_MODEL),
        'ffn2_w_down': w((L, D_FF, D_MODEL), D_FF),
    }


def reference(x, ffn1_pre_g, ffn1_post_g, ffn1_w_gate, ffn1_w_up, ffn1_w_down,
              mix_pre_g, mix_post_g, w_in, a_conv_w, a_w_out,
              b_conv_w, b_conv_b, b_ln_g, b_ln_b, b_w_out,
              c_lam_q1, c_lam_k1, c_lam_q2, c_lam_k2, c_subln_g, c_w_out,
              d_forget_b, d_w_out, w_o,
              ffn2_pre_g, ffn2_post_g, ffn2_w_gate, ffn2_w_up, ffn2_w_down):
    bsz, seq, _ = x.shape
    h = x
    for l in range(DEPTH):
        lambda_init = 0.8 - 0.6 * math.exp(-0.3 * l)
        f = swiglu(rms_norm(h, ffn1_pre_g[l]), ffn1_w_gate[l], ffn1_w_up[l], ffn1_w_down[l])
        h = h + FFN_RESIDUAL_SCALE * rms_norm(f, ffn1_post_g[l])
        u = rms_norm(h, mix_pre_g[l])
        (a_b, a_c, a_x, b_u, c_q, c_k, c_v,
         d_q, d_k, d_v, d_f, gate_logits) = _split(u @ w_in[l], IN_SPLITS)
        y_a = short_conv_mixer(a_b, a_c, a_x, a_conv_w[l], a_w_out[l])
        y_b = conformer_conv_mixer(b_u, b_conv_w[l], b_conv_b[l], b_ln_g[l], b_ln_b[l], b_w_out[l])
        y_c = diff_attention_mixer(c_q, c_k, c_v, c_lam_q1[l], c_lam_k1[l], c_lam_q2[l], c_lam_k2[l],
                                   c_subln_g[l], c_w_out[l], lambda_init)
        y_d = forgetting_attention_mixer(d_q, d_k, d_v, d_f, d_forget_b[l], d_w_out[l])
        g = jax.nn.sigmoid(gate_logits).reshape(bsz, seq, N_BRANCHES, D_MODEL)
        merged = g[:, :, 0] * y_a + g[:, :, 1] * y_b + g[:, :, 2] * y_c + g[:, :, 3] * y_d
        h = h + rms_norm(merged @ w_o[l], mix_post_g[l])
        f = swiglu(rms_norm(h, ffn2_pre_g[l]), ffn2_w_gate[l], ffn2_w_up[l], ffn2_w_down[l])
        h = h + FFN_RESIDUAL_SCALE * rms_norm(f, ffn2_post_g[l])
    return h
```

```python
import math
from contextlib import ExitStack
import numpy as np
import ml_dtypes
import concourse.bass as bass
import concourse.mybir as mybir
from concourse.bass_utils import run_bass_kernel_spmd

F32 = mybir.dt.float32
BF16 = mybir.dt.bfloat16
AF = mybir.ActivationFunctionType
ALU = mybir.AluOpType
AX = mybir.AxisListType

D = 1024
DFF = 2816
NF = DFF // 128
EPS = 1e-6
INW = 9736
O_AB, O_AC, O_AX = 0, 512, 1024
O_BU = 1536
O_CQ, O_CK, O_CV = 2560, 3072, 3584
O_DQ, O_DK, O_DV, O_DF = 4096, 4608, 5120, 5632
O_G = 5640

WNAMES = ['ffn1_w_gate', 'ffn1_w_up', 'ffn1_w_down', 'w_in', 'a_w_out', 'b_w_out', 'c_w_out',
          'd_w_out', 'w_o', 'ffn2_w_gate', 'ffn2_w_up', 'ffn2_w_down']
WSHAPE = {'ffn1_w_gate': (D, DFF), 'ffn1_w_up': (D, DFF), 'ffn1_w_down': (DFF, D), 'w_in': (D, INW),
          'a_w_out': (512, D), 'b_w_out': (512, D), 'c_w_out': (512, D), 'd_w_out': (512, D),
          'w_o': (D, D), 'ffn2_w_gate': (D, DFF), 'ffn2_w_up': (D, DFF), 'ffn2_w_down': (DFF, D)}
VNAMES = ['ffn1_pre_g', 'ffn1_post_g', 'mix_pre_g', 'mix_post_g', 'ffn2_pre_g', 'ffn2_post_g']


class Prog:
    def __init__(self):
        self.ops = []
        self.last_writer = {}
        self.readers = {}
        self.dma_counts = {}
        self.last_eng = {}
        self.last_dma = {}

    def add(self, eng, fn, reads=(), writes=(), dma=None, region=True):
        idx = len(self.ops)
        deps = set()
        if region and 'REGION' in self.last_writer:
            deps.add(self.last_writer['REGION'])
        for t in reads:
            w = self.last_writer.get(t)
            if w is not None:
                deps.add(w)
        for t in writes:
            w = self.last_writer.get(t)
            if w is not None:
                deps.add(w)
            for r in self.readers.get(t, ()):
                deps.add(r)
        deps.discard(idx)
        for t in reads:
            self.readers.setdefault(t, []).append(idx)
        for t in writes:
            self.last_writer[t] = idx
            self.readers[t] = []
        dcount = None
        if dma is not None:
            self.dma_counts[dma] = self.dma_counts.get(dma, 0) + 1
            dcount = self.dma_counts[dma]
        self.ops.append((eng, fn, deps, dma, dcount))
        if dma is None:
            self.last_eng[eng] = idx
        else:
            self.last_dma[dma] = idx
        return idx

    def barrier(self, fn):
        idx = len(self.ops)
        deps = set(self.last_eng.values()) | set(v for k, v in self.last_dma.items() if not k.startswith('cv'))
        self.ops.append(('dve', fn, deps, None, None))
        self.last_eng['dve'] = idx
        self.last_writer['REGION'] = idx
        return idx

    def emit(self, nc, block, sems, dma_sems, final_waits):
        ops = self.ops
        n = len(ops)
        milestone = [False] * n
        for i, (eng, fn, deps, dma, dc) in enumerate(ops):
            for d in deps:
                de = ops[d][0]
                if ops[d][3] is not None:
                    continue
                if de != eng or eng != 'pe':
                    milestone[d] = True
        count = {}
        mcount = [0] * n
        for i, (eng, fn, deps, dma, dc) in enumerate(ops):
            if dma is None and milestone[i]:
                count[eng] = count.get(eng, 0) + 1
                mcount[i] = count[eng]
        per_eng = {}
        for i, o in enumerate(ops):
            per_eng.setdefault(o[0], []).append(i)
        import bisect
        dma_idx = {}
        for i, o in enumerate(ops):
            if o[3] is not None:
                dma_idx.setdefault(o[3], []).append(i)

        def body(engname):
            def f(e):
                waited = {}
                for i in per_eng.get(engname, []):
                    eng, fn, deps, dma, dc = ops[i]
                    need = {}
                    for d in deps:
                        de, _, _, ddma, ddc = ops[d]
                        if ddma is not None:
                            key = ('dma', ddma)
                            val = 16 * bisect.bisect_left(dma_idx[ddma], i)
                        else:
                            if de == eng and eng == 'pe':
                                continue
                            key = ('eng', de)
                            val = mcount[d]
                        if val > need.get(key, 0):
                            need[key] = val
                    for key, val in need.items():
                        if waited.get(key, 0) >= val:
                            continue
                        waited[key] = val
                        s = dma_sems[key[1]] if key[0] == 'dma' else sems[key[1]]
                        e.wait_ge(s, val)
                    inst = fn(e)
                    if dma is not None:
                        inst.then_inc(dma_sems[dma], 16)
                    elif milestone[i]:
                        inst.then_inc(sems[eng], 1)
                if engname == 'sp':
                    for key in final_waits:
                        e.wait_ge(dma_sems[key], 16 * self.dma_counts[key])
            return f

        block.tensor(body('pe'))
        block.scalar(body('act'))
        block.vector(body('dve'))
        block.gpsimd(body('pool'))
        block.sync(body('sp'))


class Rot:
    def __init__(self, items):
        self.items = list(items)
        self.i = 0

    def next(self):
        v = self.items[self.i % len(self.items)]
        self.i += 1
        return v


def build(T, NSEQ, NL):
    NB = T // 128
    NQC = T // 512
    NT = min(1024, T)
    NPASS = T // NT
    NSUBP = NT // 512
    NBH = 8 * NB
    assert T % 512 == 0 and NBH <= 128

    nc = bass.Bass("TRN2", target_bir_lowering=False, dynamic_dma_scratch_size=1024)
    P = Prog()

    x_d = nc.dram_tensor("x", [NSEQ, T, D], F32, kind="ExternalInput").ap()
    out_d = nc.dram_tensor("out", [NSEQ, T, D], F32, kind="ExternalOutput").ap()
    W = {}
    for nm in WNAMES:
        K, N = WSHAPE[nm]
        W[nm] = nc.dram_tensor(nm, [NL, K, N], F32, kind="ExternalInput").ap()
    WS = {}
    for nm in WNAMES:
        K, N = WSHAPE[nm]
        WS[nm] = [nc.dram_tensor(f"{nm}_bf{l}", [K, N], BF16, kind="Internal").ap() for l in range(NL)]
    V = {}
    for nm in VNAMES:
        V[nm] = nc.dram_tensor(nm, [NL, D], F32, kind="ExternalInput").ap()
    a_conv_w = nc.dram_tensor("a_conv_w", [NL, 3, 512], F32, kind="ExternalInput").ap()
    b_conv_w = nc.dram_tensor("b_conv_w", [NL, 31, 512], F32, kind="ExternalInput").ap()
    b_conv_b = nc.dram_tensor("b_conv_b", [NL, 512], F32, kind="ExternalInput").ap()
    b_ln_g = nc.dram_tensor("b_ln_g", [NL, 512], F32, kind="ExternalInput").ap()
    b_ln_b = nc.dram_tensor("b_ln_b", [NL, 512], F32, kind="ExternalInput").ap()
    lamv = {}
    for nm in ['c_lam_q1', 'c_lam_k1', 'c_lam_q2', 'c_lam_k2']:
        lamv[nm] = nc.dram_tensor(nm, [NL, 64], F32, kind="ExternalInput").ap()
    c_subln_g = nc.dram_tensor("c_subln_g", [NL, 128], F32, kind="ExternalInput").ap()
    d_forget_b = nc.dram_tensor("d_forget_b", [NL, 8], F32, kind="ExternalInput").ap()
    cf32_d = nc.dram_tensor("cf32", [128, 4, 128], F32, kind="ExternalInput").ap()
    cbf_d = nc.dram_tensor("cbf", [128, 3, 128], BF16, kind="ExternalInput").ap()
    alibi_d = nc.dram_tensor("alibi", [4, 2, 4, T], BF16, kind="ExternalInput").ap()
    lamc_d = nc.dram_tensor("lamc", [NL, 2], F32, kind="ExternalInput").ap()
    p3_d = nc.dram_tensor("p3_d", [128, 6, 128], BF16, kind="Internal").ap()

    off = [(int(nc.sbuf_base) + 255) // 256 * 256]

    def alloc(name, shape, dt, at=None):
        nbytes = int(np.prod(shape[1:])) * (4 if dt == F32 else 2)
        nbytes = (nbytes + 63) // 64 * 64
        if at is None:
            o = off[0]
            off[0] += nbytes
        else:
            o = at
        assert o + nbytes <= 229344, (name, o, nbytes)
        return nc.alloc_sbuf_tensor_at(name, list(shape), dt, offset=o), o + nbytes

    hT, _ = alloc("hT", [128, 8, T], F32)
    cf32, _ = alloc("cf32s", [128, 4, 128], F32)
    cbf, _ = alloc("cbfs", [128, 3, 128], BF16)
    ident = cf32[:, 0, :]
    ones_f = cf32[:, 1, :]
    U_f = cf32[:, 2, :]
    E_f = cf32[:, 3, :]
    ones_b = cbf[:, 0, :]
    maskT = cbf[:, 1, :]
    ident_b = cbf[:, 2, :]
    gains, _ = alloc("gains", [128, NL, 6, 8], F32)
    ghalf, _ = alloc("ghalf", [128, NL, 2, 8], F32)
    acw, _ = alloc("acw", [128, NL, 4, 3], F32)
    bcw, _ = alloc("bcw", [128, NL, 4, 31], F32)
    bvec, _ = alloc("bvec", [128, NL, 3, 4], F32)
    subg, _ = alloc("subg", [128, NL], F32)
    neglam, _ = alloc("neglam", [128, NL], F32)
    lamt2, _ = alloc("lamt2", [128, NL, 8], F32)
    fb16, _ = alloc("fb16", [128, NL, 128], F32)
    fb8, _ = alloc("fb8", [128, NL, 8], F32)
    dummy, _ = alloc("dummy", [128, 16], F32)
    NSLOT = 4
    wslot = []
    for i in range(NSLOT):
        t_, _ = alloc(f"wslot{i}", [128, 4096], BF16)
        wslot.append(t_)
    wf_t, _ = alloc("wf", [128, 8, 8], BF16)
    rs_t = []
    for i in range(2):
        t_, _ = alloc(f"rstd{i}", [128, 512], F32)
        rs_t.append(t_)
    lnv_t, _ = alloc("lnv", [128, 512], F32)
    sq_t = []
    for i in range(4):
        t_, _ = alloc(f"sq{i}", [128, 512], BF16)
        sq_t.append(t_)
    PH = off[0]
    o = PH
    lamtmp, o = alloc("lamtmp", [128, NL, 4, 64], F32, at=o)
    o = PH
    xn, o = alloc("xn", [128, 8, NT], BF16, at=o)
    actT, o = alloc("actT", [128, NF, NT], BF16, at=o)
    fT, o = alloc("fT", [128, 8, NT], F32, at=o)
    s_t = []
    for i in range(2):
        t_, o = alloc(f"sil{i}", [128, 512], F32, at=o)
        s_t.append(t_)
    tmpx = []
    for i in range(2):
        t_, o = alloc(f"tmpx{i}", [128, 512], F32, at=o)
        tmpx.append(t_)
    xs_t = []
    o2 = PH + 8 * NT * 2
    for i in range(2):
        t_, o2 = alloc(f"xs{i}", [128, 1024], F32, at=o2)
        xs_t.append(t_)
    o = PH
    uT, o = alloc("uT", [128, 8, T], BF16, at=o)
    cdT, o = alloc("cdT", [128, 8, T], BF16, at=o)
    MX = o
    vt, o = alloc("vt", [128, NB, 512], BF16, at=o)
    qa, ka = [], []
    for i in range(2):
        t_, o = alloc(f"qa{i}", [128, T], BF16, at=o)
        qa.append(t_)
        t_, o = alloc(f"ka{i}", [128, T], BF16, at=o)
        ka.append(t_)
    pt_t = []
    for i in range(4):
        t_, o = alloc(f"pt{i}", [128, 512], BF16, at=o)
        pt_t.append(t_)
    ot_t = []
    for i in range(3):
        t_, o = alloc(f"ot{i}", [128, 512], F32, at=o)
        ot_t.append(t_)
    cz, ce, clf, ctb = (ot_t[2][:, 0:128], ot_t[2][:, 128:256], ot_t[2][:, 256:384], ot_t[2][:, 384:512])
    c8, o = alloc("c8", [128, 128], F32, at=o)
    cr1, o = alloc("cr1", [128, 128], F32, at=o)
    cr2, o = alloc("cr2", [128, 128], F32, at=o)
    p3, o = alloc("p3", [128, 6, 128], BF16, at=o)
    o = MX
    zA, o = alloc("zA", [128, 4, 514], F32, at=o)
    yglu, o = alloc("yglu", [128, 4, 542], BF16, at=o)
    yB, o = alloc("yB", [128, 4, 512], F32, at=o)
    dg_t = []
    for i in range(8):
        t_, o = alloc(f"dg{i}", [128, 128], BF16, at=o)
        dg_t.append(t_)
    MX2 = o
    fTm, _ = alloc("fTm", [128, 8, 512], F32, at=MX)
    abT, o = alloc("abT", [128, 8, 512], BF16, at=max(MX2, MX + 8 * 512 * 4 + 8 * 512 * 2))
    mrg, _ = alloc("mrg", [128, 8, 512], BF16, at=MX + 8 * 512 * 4)
    sg_t = []
    for i in range(2):
        t_, o = alloc(f"sg{i}", [128, 512], F32, at=o)
        sg_t.append(t_)
    tmpm, o = alloc("tmpm", [128, 512], F32, at=o)
    macc, o = alloc("macc", [128, 512], F32, at=o)
    zAtail, o = alloc("zAtail", [128, 4, 2], F32, at=o)
    ygtail, o = alloc("ygtail", [128, 4, 30], BF16, at=o)
    print("SBUF persistent end", PH, "mixer end", o)

    es = ExitStack()
    ps = [es.enter_context(nc.psum_tensor(f"ps{i}", [128, 512], F32)) for i in range(8)]
    PSK = [f"ps{i}" for i in range(8)]

    def dma(out, in_, reads, writes, key):
        P.add('sp', lambda e, out=out, in_=in_: e.dma_start(out=out, in_=in_), reads, writes, dma=key)

    def mm(out, lhsT, rhs, start, stop, reads, writes):
        P.add('pe', lambda e, out=out, lhsT=lhsT, rhs=rhs, start=start, stop=stop:
              e.matmul(out, lhsT, rhs, start=start, stop=stop), reads, writes)

    def act(out, in_, func, reads, writes, bias=None, scale=None):
        kw = {}
        if bias is not None:
            kw['bias'] = bias
        if scale is not None:
            kw['scale'] = scale
        P.add('act', lambda e, out=out, in_=in_, func=func, kw=kw: e.activation(out, in_, func, **kw), reads, writes)

    def tt(eng, out, in0, in1, op, reads, writes):
        P.add(eng, lambda e, out=out, in0=in0, in1=in1, op=op: e.tensor_tensor(out, in0, in1, op), reads, writes)

    def ts(eng, out, in0, s1, s2, op0, op1, reads, writes):
        if op1 is None:
            P.add(eng, lambda e, out=out, in0=in0, s1=s1, op0=op0: e.tensor_scalar(out, in0, s1, None, op0), reads, writes)
        else:
            P.add(eng, lambda e, out=out, in0=in0, s1=s1, s2=s2, op0=op0, op1=op1:
                  e.tensor_scalar(out, in0, s1, s2, op0, op1), reads, writes)

    def stt(out, in0, scalar, in1, op0, op1, reads, writes):
        P.add('dve', lambda e, out=out, in0=in0, scalar=scalar, in1=in1, op0=op0, op1=op1:
              e.scalar_tensor_tensor(out, in0, scalar, in1, op0, op1), reads, writes)

    def cp(eng, out, in_, reads, writes):
        if eng == 'act':
            P.add('act', lambda e, out=out, in_=in_: e.copy(out, in_), reads, writes)
        else:
            P.add(eng, lambda e, out=out, in_=in_: e.tensor_copy(out, in_), reads, writes)

    dma(cf32[:], cf32_d, [], [f'cst_{len(P.ops)}'], 'c0')
    dma(cbf[:], cbf_d, [], [f'cst_{len(P.ops)}'], 'c0')
    for l in range(NL):
        for i, nm in enumerate(VNAMES):
            dma(gains[:, l, i, :], V[nm][l].rearrange("(c p) -> p c", p=128), [], [f'cst_{len(P.ops)}'], 'c0')
        for cc in range(4):
            dma(acw[:, l, cc, :], a_conv_w[l][:, cc * 128:(cc + 1) * 128].rearrange("k p -> p k"), [], [f'cst_{len(P.ops)}'], 'c0')
            dma(bcw[:, l, cc, :], b_conv_w[l][:, cc * 128:(cc + 1) * 128].rearrange("k p -> p k"), [], [f'cst_{len(P.ops)}'], 'c0')
        for i, v_ in enumerate([b_conv_b, b_ln_g, b_ln_b]):
            dma(bvec[:, l, i, :], v_[l].rearrange("(c p) -> p c", p=128), [], [f'cst_{len(P.ops)}'], 'c0')
        dma(subg[:, l:l + 1], c_subln_g[l].rearrange("(p o) -> p o", o=1), [], [f'cst_{len(P.ops)}'], 'c0')
        dma(fb8[:, l, :], d_forget_b[l:l + 1, :].broadcast_to([128, 8]), [], [f'cst_{len(P.ops)}'], 'c0')
        for i, nm in enumerate(['c_lam_q1', 'c_lam_k1', 'c_lam_q2', 'c_lam_k2']):
            dma(lamtmp[:, l, i, :], lamv[nm][l:l + 1, :].broadcast_to([128, 64]), [], [f'cst_{len(P.ops)}'], 'c0')
        dma(lamt2[:, l, 6:8], lamc_d[l:l + 1, :].broadcast_to([128, 2]), [], [f'cst_{len(P.ops)}'], 'c0')

    NSETUP = len(P.ops)
    P.add('dve', lambda e: e.memset(dummy[:], 0.0), [f'cst_{i}' for i in range(NSETUP)], ['const'])
    for l in range(NL):
        cp('dve', fb16[:, l, :].rearrange("p (h j) -> p h j", j=16),
           fb8[:, l, :].unsqueeze(2).broadcast_to([128, 8, 16]), ['const'], ['const2'])
    for l in range(NL):
        ts('dve', ghalf[:, l, 0, :], gains[:, l, 1, :], 0.5, None, ALU.mult, None, ['const'], ['const2'])
        ts('dve', ghalf[:, l, 1, :], gains[:, l, 5, :], 0.5, None, ALU.mult, None, ['const'], ['const2'])
        lt, l2 = lamtmp[:, l], lamt2[:, l]
        tt('dve', lt[:, 0, :], lt[:, 0, :], lt[:, 1, :], ALU.mult, ['const'], [f'lamtmp0_{l}'])
        tt('dve', lt[:, 2, :], lt[:, 2, :], lt[:, 3, :], ALU.mult, ['const'], [f'lamtmp2_{l}'])
        P.add('dve', lambda e, lt=lt, l2=l2: e.reduce_sum(l2[:, 0:1], lt[:, 0, :], AX.X), [f'lamtmp0_{l}'], [f'lt0_{l}'])
        P.add('dve', lambda e, lt=lt, l2=l2: e.reduce_sum(l2[:, 1:2], lt[:, 2, :], AX.X), [f'lamtmp2_{l}'], [f'lt1_{l}'])
        act(l2[:, 2:4], l2[:, 0:2], AF.Exp, [f'lt0_{l}', f'lt1_{l}'], [f'lt2_{l}'])
        tt('dve', l2[:, 4:5], l2[:, 3:4], l2[:, 2:3], ALU.subtract, [f'lt2_{l}'], [f'lt4_{l}'])
        tt('dve', neglam[:, l:l + 1], l2[:, 4:5], l2[:, 6:7], ALU.subtract, [f'lt4_{l}', 'const'], ['const2'])
        tt('dve', subg[:, l:l + 1], subg[:, l:l + 1], l2[:, 7:8], ALU.mult, ['const'], ['const2'])

    CONVTOK = {}

    def convert(nm, l):
        K, N = WSHAPE[nm]
        key = f'cv{l}_{nm}'
        toks = []
        for c in range(K // 128):
            for n0 in range(0, N, 2048):
                n1 = min(N, n0 + 2048)
                tok = f'wscr{len(P.ops)}'
                P.add('pool', lambda e, o_=WS[nm][l][c * 128:(c + 1) * 128, n0:n1], i_=W[nm][l, c * 128:(c + 1) * 128, n0:n1]:
                      e.dma_start(out=o_, in_=i_), [], [tok], dma=key, region=False)
                toks.append(tok)
        CONVTOK[f"{nm}_bf{l}"] = toks

    CGROUPS = []
    for l in range(NL):
        CGROUPS.append([('ffn1_w_gate', l), ('ffn1_w_up', l), ('ffn1_w_down', l)])
        CGROUPS.append([('w_in', l), ('a_w_out', l), ('b_w_out', l), ('c_w_out', l), ('d_w_out', l), ('w_o', l)])
        CGROUPS.append([('ffn2_w_gate', l), ('ffn2_w_up', l), ('ffn2_w_down', l)])
    cg_next = [0]

    def convert_next_group():
        if cg_next[0] < len(CGROUPS):
            for (nm, l) in CGROUPS[cg_next[0]]:
                convert(nm, l)
            cg_next[0] += 1

    streamed = set()

    srot = Rot(range(NSLOT))

    def wload(src_ap, kc, ncols):
        i = srot.next()
        view = wslot[i][:, 0:kc * ncols].rearrange("p (c n) -> p c n", n=ncols)
        reads = []
        nm = src_ap.name
        if nm not in streamed:
            reads = CONVTOK[nm]
            streamed.add(nm)
        P.add('sp', lambda e, out=view, in_=src_ap.rearrange("(c p) n -> p c n", p=128): e.dma_start(out=out, in_=in_),
              reads, [f'wslot{i}'], dma=f'ws{i}', region=False)
        return view, f'wslot{i}'

    sqrot = Rot(range(4))

    def norm_stats(src_fn, src_tok_fn, nchunks, t0, ss_bank, denom, rs_idx):
        for c in range(nchunks):
            i = sqrot.next()
            if c % 2 == 0:
                tt('pool', sq_t[i][:], src_fn(c), src_fn(c), ALU.mult, src_tok_fn(c), [f'sq{i}'])
            else:
                act(sq_t[i][:], src_fn(c), AF.Square, src_tok_fn(c), [f'sq{i}'])
            mm(ps[ss_bank][:], ones_b, sq_t[i][:], c == 0, c == nchunks - 1, [f'sq{i}', 'const'], [PSK[ss_bank]])
        act(lnv_t[:], ps[ss_bank][:], AF.Ln, [PSK[ss_bank]], ['lnv'], bias=EPS, scale=1.0 / denom)
        act(rs_t[rs_idx][:], lnv_t[:], AF.Exp, ['lnv'], [f'rstd{rs_idx}'], scale=-0.5)

    def prenorm_stats(ts0, bank):
        tsub = ts0 // 512
        for c in range(8):
            i = sqrot.next()
            src = hT[:, c, ts0:ts0 + 512]
            if c % 2 == 0:
                tt('pool', sq_t[i][:], src, src, ALU.mult, [f'hT{c}_{tsub}'], [f'sq{i}'])
            else:
                act(sq_t[i][:], src, AF.Square, [f'hT{c}_{tsub}'], [f'sq{i}'])
            mm(ps[bank][:], ones_b, sq_t[i][:], c == 0, c == 7, [f'sq{i}', 'const'], [PSK[bank]])

    def prenorm_scale(l, gi, dst_fn, dst_tok_fn, ts0, bank, ri, tmps):
        tsub = ts0 // 512
        act(lnv_t[:], ps[bank][:], AF.Ln, [PSK[bank]], ['lnv'], bias=EPS, scale=1.0 / D)
        act(rs_t[ri][:], lnv_t[:], AF.Exp, ['lnv'], [f'rstd{ri}'], scale=-0.5)
        for c in range(8):
            if c % 2 == 0:
                stt(dst_fn(c), hT[:, c, ts0:ts0 + 512], gains[:, l, gi, c:c + 1], rs_t[ri][:], ALU.mult, ALU.mult,
                    [f'hT{c}_{tsub}', f'rstd{ri}', 'const'], [dst_tok_fn(c)])
            else:
                tmp, ttok = tmps[(c // 2) % 2]
                tt('pool', tmp[:], hT[:, c, ts0:ts0 + 512], rs_t[ri][:], ALU.mult, [f'hT{c}_{tsub}', f'rstd{ri}'], [ttok])
                act(dst_fn(c), tmp[:], AF.Identity, [ttok, 'const'], [dst_tok_fn(c)], scale=gains[:, l, gi, c:c + 1])

    def ffn(l, which):
        gi_pre, hi = (0, 0) if which == 1 else (4, 1)
        pfx = 'ffn1' if which == 1 else 'ffn2'
        wg, wu, wd = WS[pfx + '_w_gate'][l], WS[pfx + '_w_up'][l], WS[pfx + '_w_down'][l]
        brot = Rot(range(6))
        srt = Rot(range(2))
        tmps = [(tmpx[0], 'tmpx0'), (tmpx[1], 'tmpx1')]

        def pre(pss):
            banks = [brot.next() for _ in range(NSUBP)]
            for sub in range(NSUBP):
                prenorm_stats(pss * NT + sub * 512, banks[sub])
            for sub in range(NSUBP):
                prenorm_scale(l, gi_pre, lambda c, sub=sub: xn[:, c, sub * 512:(sub + 1) * 512],
                              lambda c, sub=sub: f'xn{c}_{sub}', pss * NT + sub * 512, banks[sub], sub % 2, tmps)

        pre(0)
        for pss in range(NPASS):
            t0 = pss * NT
            for fg in range(0, NF, 4):
                nfg = min(4, NF - fg)
                wgv, wgt = wload(wg[:, fg * 128:(fg + nfg) * 128], 8, nfg * 128)
                wuv, wut = wload(wu[:, fg * 128:(fg + nfg) * 128], 8, nfg * 128)
                for sub in range(NSUBP):
                    for fi in range(nfg):
                        f = fg + fi
                        bg, bu = brot.next(), brot.next()
                        for c in range(8):
                            mm(ps[bg][:], wgv[:, c, fi * 128:(fi + 1) * 128], xn[:, c, sub * 512:(sub + 1) * 512],
                               c == 0, c == 7, [wgt, f'xn{c}_{sub}'], [PSK[bg]])
                        for c in range(8):
                            mm(ps[bu][:], wuv[:, c, fi * 128:(fi + 1) * 128], xn[:, c, sub * 512:(sub + 1) * 512],
                               c == 0, c == 7, [wut, f'xn{c}_{sub}'], [PSK[bu]])
                        si = srt.next()
                        act(s_t[si][:], ps[bg][:], AF.Silu, [PSK[bg]], [f'sil{si}'])
                        tt('dve', actT[:, f, sub * 512:(sub + 1) * 512], s_t[si][:], ps[bu][:], ALU.mult,
                           [f'sil{si}', PSK[bu]], [f'act{f}_{sub}'])
            if pss + 1 < NPASS:
                pre(pss + 1)
            pend = []
            for oc in range(8):
                wdv, wdt = wload(wd[:, oc * 128:(oc + 1) * 128], NF, 128)
                for sub in range(NSUBP):
                    b = brot.next()
                    for f in range(NF):
                        mm(ps[b][:], wdv[:, f, :], actT[:, f, sub * 512:(sub + 1) * 512], f == 0, f == NF - 1,
                           [wdt, f'act{f}_{sub}'], [PSK[b]])
                    for (pi_, psub, poc) in pend:
                        mm(ps[6 + psub][:], ones_b, sq_t[pi_][:], poc == 0, poc == 7, [f'sq{pi_}', 'const'], [PSK[6 + psub]])
                    pend = []
                    cp('act', fT[:, oc, sub * 512:(sub + 1) * 512], ps[b][:], [PSK[b]], [f'fT{oc}_{sub}'])
                    i = sqrot.next()
                    tt('pool', sq_t[i][:], fT[:, oc, sub * 512:(sub + 1) * 512], fT[:, oc, sub * 512:(sub + 1) * 512],
                       ALU.mult, [f'fT{oc}_{sub}'], [f'sq{i}'])
                    pend.append((i, sub, oc))
            for (pi_, psub, poc) in pend:
                mm(ps[6 + psub][:], ones_b, sq_t[pi_][:], poc == 0, poc == 7, [f'sq{pi_}', 'const'], [PSK[6 + psub]])
            for sub in range(NSUBP):
                ts0 = t0 + sub * 512
                tsub = ts0 // 512
                act(lnv_t[:], ps[6 + sub][:], AF.Ln, [PSK[6 + sub]], ['lnv'], bias=EPS, scale=1.0 / D)
                act(rs_t[sub % 2][:], lnv_t[:], AF.Exp, ['lnv'], [f'rstd{sub % 2}'], scale=-0.5)
                for oc in range(8):
                    stt(fT[:, oc, sub * 512:(sub + 1) * 512], fT[:, oc, sub * 512:(sub + 1) * 512],
                        ghalf[:, l, hi, oc:oc + 1], rs_t[sub % 2][:], ALU.mult, ALU.mult,
                        [f'fT{oc}_{sub}', f'rstd{sub % 2}', 'const2'], [f'fT{oc}_{sub}'])
                    tt('pool', hT[:, oc, ts0:ts0 + 512], hT[:, oc, ts0:ts0 + 512], fT[:, oc, sub * 512:(sub + 1) * 512],
                       ALU.add, [f'fT{oc}_{sub}', f'hT{oc}_{tsub}'], [f'hT{oc}_{tsub}'])

    def barrier():
        P.barrier(lambda e: e.memset(dummy[:], 0.0))

    def mixer(l):
        win = WS['w_in'][l]
        for qc in range(NQC):
            prenorm_stats(qc * 512, 4 + qc % 4)
        for qc in range(NQC):
            prenorm_scale(l, 2, lambda c, qc=qc: uT[:, c, qc * 512:(qc + 1) * 512], lambda c, qc=qc: f'uT{qc}',
                          qc * 512, 4 + qc % 4, qc % 2, [(ot_t[0], 'o1'), (ot_t[1], 'o2')])
        strot = Rot([0, 1, 2, 3])
        prot = strot
        ptrot = Rot(range(4))
        odrot = Rot([(4, 5), (6, 7)])

        def proj_fm(wv, wt, col0, ncol, dst_fn, dst_tok):
            for qc in range(NQC):
                b = prot.next()
                for c in range(8):
                    mm(ps[b][0:ncol, :], wv[:, c, col0:col0 + ncol], uT[:, c, qc * 512:(qc + 1) * 512],
                       c == 0, c == 7, [wt, f'uT{qc}'], [PSK[b]])
                cp('dve', dst_fn(qc), ps[b][0:ncol, :], [PSK[b]], [dst_tok])

        def v_tokmajor(wvv, wvt):
            for j in range(NB):
                b = prot.next()
                for c in range(8):
                    mm(ps[b][:], uT[:, c, j * 128:(j + 1) * 128], wvv[:, c, :], c == 0, c == 7,
                       [wvt, f'uT{j // 4}'], [PSK[b]])
                cp('dve', vt[:, j, :], ps[b][:], [PSK[b]], ['vt'])

        def attention_qc(qt, kt, nk, v_fn, o_rows, fin_fn, qtok, ktok, qc):
            r0, r1 = o_rows
            bo, bd = odrot.next()
            njb = 4 * qc + 4
            LOOK = 3

            def geom(j):
                qlo = max(qc * 512, j * 128)
                return qlo, (qc + 1) * 512 - qlo, qlo - qc * 512

            def issue_qk(j):
                qlo, w, c0 = geom(j)
                sb = strot.next()
                mm(ps[sb][:, 0:w], kt[0:nk, j * 128:(j + 1) * 128], qt[0:nk, qlo:qlo + w], True, True,
                   [qtok, ktok], [PSK[sb]])
                pi = ptrot.next()
                act(pt_t[pi][:, 0:w], ps[sb][:, 0:w], AF.Exp, [PSK[sb]], [f'pt{pi}'], scale=0.125)
                if j >= 4 * qc:
                    tt('pool', pt_t[pi][:, 0:128], pt_t[pi][:, 0:128], maskT, ALU.mult, [f'pt{pi}', 'const'], [f'pt{pi}'])
                return pi

            def issue_pv(j, pi):
                qlo, w, c0 = geom(j)
                mm(ps[bo][r0:r1, c0:512], v_fn(j), pt_t[pi][:, 0:w], j == 0, j == njb - 1, ['vt', f'pt{pi}'], [PSK[bo]])
                mm(ps[bd][:, c0:512], ones_b, pt_t[pi][:, 0:w], j == 0, j == njb - 1, ['const', f'pt{pi}'], [PSK[bd]])

            pis = []
            for j in range(njb):
                pis.append(issue_qk(j))
                if j >= LOOK:
                    issue_pv(j - LOOK, pis[j - LOOK])
            for j in range(max(0, njb - LOOK), njb):
                issue_pv(j, pis[j])
            fin_fn(qc, bo, bd)

        def proj_fm2(wv, wt, col0, dst0_fn, dst1_fn, tok0, tok1):
            for qc in range(NQC):
                b = prot.next()
                for c in range(8):
                    mm(ps[b][:], wv[:, c, col0:col0 + 128], uT[:, c, qc * 512:(qc + 1) * 512],
                       c == 0, c == 7, [wt, f'uT{qc}'], [PSK[b]])
                cp('dve', dst0_fn(qc), ps[b][0:64, :], [PSK[b]], [tok0])
                cp('act', dst1_fn(qc), ps[b][64:128, :], [PSK[b]], [tok1])

        cdefer = []
        wqv, wqt = wload(win[:, O_CQ:O_CQ + 512], 8, 512)
        wkv, wkt = wload(win[:, O_CK:O_CK + 512], 8, 512)
        wvv, wvt = wload(win[:, O_CV:O_CV + 512], 8, 512)
        v_tokmajor(wvv, wvt)
        for hc in range(4):
            for m in range(2):
                dma(qa[m][64:68, :], alibi_d[hc, 0], [], [f'qa{m}'], f'qa{m}')
                dma(ka[m][64:68, :], alibi_d[hc, 1], [], [f'ka{m}'], f'ka{m}')
            proj_fm2(wqv, wqt, hc * 128, lambda qc: qa[0][0:64, qc * 512:(qc + 1) * 512],
                     lambda qc: qa[1][0:64, qc * 512:(qc + 1) * 512], 'qa0', 'qa1')
            proj_fm2(wkv, wkt, hc * 128, lambda qc: ka[0][0:64, qc * 512:(qc + 1) * 512],
                     lambda qc: ka[1][0:64, qc * 512:(qc + 1) * 512], 'ka0', 'ka1')
            for qc in range(NQC):
                ri = qc % 2

                def fin0(qc, bo, bd, ri=ri):
                    while cdefer:
                        cdefer.pop(0)()
                    act(lnv_t[:], ps[bd][:], AF.Ln, [PSK[bd]], ['lnv'])
                    act(rs_t[ri][:], lnv_t[:], AF.Exp, ['lnv'], [f'rstd{ri}'], scale=-1.0)
                    tt('dve', ot_t[0][:], ps[bo][:], rs_t[ri][:], ALU.mult, [PSK[bo], f'rstd{ri}'], ['o1'])

                def fin1(qc, bo, bd, ri=ri, hc=hc):
                    act(lnv_t[:], ps[bd][:], AF.Ln, [PSK[bd]], ['lnv'])
                    act(rs_t[ri][:], lnv_t[:], AF.Exp, ['lnv'], [f'rstd{ri}'], scale=-1.0)
                    tt('dve', ot_t[1][:], ps[bo][:], rs_t[ri][:], ALU.mult, [PSK[bo], f'rstd{ri}'], ['o2'])
                    stt(ot_t[2][:], ot_t[1][:], neglam[:, l:l + 1], ot_t[0][:], ALU.mult, ALU.add,
                        ['o2', 'o1', 'const2'], ['od'])
                    i = sqrot.next()
                    tt('pool', sq_t[i][:], ot_t[2][:], ot_t[2][:], ALU.mult, ['od'], [f'sq{i}'])

                    def later(i=i, ri=ri, hc=hc, qc=qc):
                        b = prot.next()
                        mm(ps[b][:], ones_b, sq_t[i][:], True, True, [f'sq{i}', 'const'], [PSK[b]])
                        act(lnv_t[:], ps[b][:], AF.Ln, [PSK[b]], ['lnv'], bias=EPS, scale=1.0 / 128)
                        act(rs_t[ri][:], lnv_t[:], AF.Exp, ['lnv'], [f'rstd{ri}'], scale=-0.5)
                        stt(cdT[:, hc, qc * 512:(qc + 1) * 512], ot_t[2][:], subg[:, l:l + 1], rs_t[ri][:],
                            ALU.mult, ALU.mult, ['od', f'rstd{ri}', 'const2'], [f'cd{hc}_{qc}'])
                    cdefer.append(later)

                vf = lambda j, hc=hc: vt[:, j, hc * 128:(hc + 1) * 128]
                attention_qc(qa[0], ka[0], 68, vf, (0, 128), fin0, 'qa0', 'ka0', qc)
                attention_qc(qa[1], ka[1], 68, vf, (0, 128), fin1, 'qa1', 'ka1', qc)

        while cdefer:
            cdefer.pop(0)()
        wqv, wqt = wload(win[:, O_DQ:O_DQ + 512], 8, 512)
        wkv, wkt = wload(win[:, O_DK:O_DK + 512], 8, 512)
        wvv, wvt = wload(win[:, O_DV:O_DV + 512], 8, 512)
        P.add('sp', lambda e: e.dma_start(out=wf_t[:], in_=win[:, O_DF:O_DF + 8].rearrange("(c p) n -> p c n", p=128)),
              [], ['wf'], dma='wf', region=False)
        v_tokmajor(wvv, wvt)
        b = prot.next()
        for j in range(NB):
            for c in range(8):
                mm(ps[b][:, j * 8:(j + 1) * 8], uT[:, c, j * 128:(j + 1) * 128], wf_t[:, c, :], c == 0, c == 7,
                   ['wf', f'uT{j // 4}'], [PSK[b]])
        if NBH < 128:
            P.add('dve', lambda e: e.memset(cz, 30.0), [], ['cz', 'od'])
        tt('dve', cz[:, 0:NBH].rearrange("p (h j) -> p h j", j=NB),
           ps[b][:, 0:NBH].rearrange("p (j h) -> p h j", h=8),
           fb16[:, l, :].rearrange("p (h j) -> p h j", j=16)[:, :, 0:NB], ALU.add, [PSK[b], 'const2'], ['cz', 'od'])
        act(ce, cz, AF.Exp, ['cz'], ['ce'], scale=-1.0)
        act(clf, ce, AF.Ln, ['ce'], ['clf'], bias=1.0)
        b1 = prot.next()
        mm(ps[b1][:, 0:128], clf, ones_f, True, True, ['clf', 'const'], [PSK[b1]])
        cp('dve', ctb, ps[b1][:, 0:128], [PSK[b1]], ['ctb'])
        b2 = prot.next()
        mm(ps[b2][:, 0:128], clf, U_f, True, False, ['clf', 'const'], [PSK[b2]])
        mm(ps[b2][:, 0:128], E_f, ctb, False, True, ['ctb', 'const'], [PSK[b2]])
        ts('dve', c8[:], ps[b2][:, 0:128], 8.0, None, ALU.mult, None, [PSK[b2]], ['c8'])
        cp('dve', p3[:, 0, :], c8[:], ['c8'], ['p3a'])
        tt('dve', cr1[:], c8[:], p3[:, 0, :], ALU.subtract, ['c8', 'p3a'], ['cr1'])
        cp('dve', p3[:, 1, :], cr1[:], ['cr1'], ['p3b'])
        tt('dve', cr2[:], cr1[:], p3[:, 1, :], ALU.subtract, ['cr1', 'p3b'], ['cr2'])
        cp('dve', p3[:, 2, :], cr2[:], ['cr2'], ['p3c'])
        ts('dve', p3[:, 3:6, :], p3[:, 0:3, :], -1.0, None, ALU.mult, None, ['p3a', 'p3b', 'p3c'], ['p3n'])
        dma(p3_d[0:NBH], p3[0:NBH], ['p3a', 'p3b', 'p3c', 'p3n'], ['p3d'], 'p3d')
        P3TOK = ['p3d']
        for i in range(2):
            P.add('dve', lambda e, i=i: e.memset(qa[i][64:70, :], 1.0), [f'qa{i}'], [f'qa{i}'])
            P.add('dve', lambda e, i=i: e.memset(ka[i][64:70, :], 1.0), [f'ka{i}'], [f'ka{i}'])
        for hd in range(8):
            qi = hd % 2
            if hd % 2 == 0:
                dma(qa[0][64:67, :].rearrange("r (j k) -> r j k", k=128),
                    p3_d[hd * NB:(hd + 1) * NB, 3:6, :].rearrange("j r k -> r j k"), P3TOK, ['qa0'], 'qa0')
                dma(ka[0][67:70, :].rearrange("r (j k) -> r j k", k=128),
                    p3_d[hd * NB:(hd + 1) * NB, 0:3, :].rearrange("j r k -> r j k"), P3TOK, ['ka0'], 'ka0')
            if hd % 2 == 0:
                dma(qa[1][64:67, :].rearrange("r (j k) -> r j k", k=128),
                    p3_d[(hd + 1) * NB:(hd + 2) * NB, 3:6, :].rearrange("j r k -> r j k"), P3TOK, ['qa1'], 'qa1')
                dma(ka[1][67:70, :].rearrange("r (j k) -> r j k", k=128),
                    p3_d[(hd + 1) * NB:(hd + 2) * NB, 0:3, :].rearrange("j r k -> r j k"), P3TOK, ['ka1'], 'ka1')
                proj_fm2(wqv, wqt, hd * 64, lambda qc: qa[0][0:64, qc * 512:(qc + 1) * 512],
                         lambda qc: qa[1][0:64, qc * 512:(qc + 1) * 512], 'qa0', 'qa1')
                proj_fm2(wkv, wkt, hd * 64, lambda qc: ka[0][0:64, qc * 512:(qc + 1) * 512],
                         lambda qc: ka[1][0:64, qc * 512:(qc + 1) * 512], 'ka0', 'ka1')
            r0 = (hd % 2) * 64

            def find(qc, bo, bd, hd=hd, r0=r0):
                ri = qc % 2
                act(lnv_t[:], ps[bd][:], AF.Ln, [PSK[bd]], ['lnv'])
                act(rs_t[ri][:], lnv_t[:], AF.Exp, ['lnv'], [f'rstd{ri}'], scale=-1.0)
                tt('dve', cdT[r0:r0 + 64, 4 + hd // 2, qc * 512:(qc + 1) * 512], ps[bo][r0:r0 + 64, :],
                   rs_t[ri][r0:r0 + 64, :], ALU.mult, [PSK[bo], f'rstd{ri}'], [f'cd{4 + hd // 2}_{qc}'])

            for qc in range(NQC):
                attention_qc(qa[qi], ka[qi], 70, lambda j, hd=hd: vt[:, j, (hd // 2) * 128:(hd // 2) * 128 + 128], (0, 128),
                             find, f'qa{qi}', f'ka{qi}', qc)

        barrier()
        brot = Rot(range(6))
        sgrot = Rot(range(2))
        for sub in range(NQC):
            ts0 = sub * 512
            UTK = f'uT{sub}'
            wbv, wbt = wload(win[:, O_AB:O_AB + 512], 8, 512)
            wcv, wct = wload(win[:, O_AC:O_AC + 512], 8, 512)
            wxv, wxt = wload(win[:, O_AX:O_AX + 512], 8, 512)
            for cc in range(4):
                bb, bc_, bx = brot.next(), brot.next(), brot.next()
                for (bk, wv_, wt_) in ((bb, wbv, wbt), (bc_, wcv, wct), (bx, wxv, wxt)):
                    for c in range(8):
                        mm(ps[bk][:], wv_[:, c, cc * 128:(cc + 1) * 128], uT[:, c, ts0:ts0 + 512], c == 0, c == 7,
                           [wt_, UTK], [PSK[bk]])
                if sub == 0:
                    P.add('dve', lambda e, cc=cc: e.memset(zA[:, cc, 0:2], 0.0), [f'zA{cc}'], [f'zA{cc}'])
                else:
                    cp('dve', zA[:, cc, 0:2], zAtail[:, cc, :], [f'zAt{cc}'], [f'zA{cc}'])
                cp('act', sg_t[0][:], ps[bc_][:], [PSK[bc_]], ['sg0'])
                tt('dve', zA[:, cc, 2:514], sg_t[0][:], ps[bx][:], ALU.mult, ['sg0', PSK[bx]], [f'zA{cc}'])
                if sub < NQC - 1:
                    cp('pool', zAtail[:, cc, :], zA[:, cc, 512:514], [f'zA{cc}'], [f'zAt{cc}'])
                ts('dve', tmpm[:], zA[:, cc, 2:514], acw[:, l, cc, 2:3], None, ALU.mult, None, [f'zA{cc}', 'const'], ['tmpm'])
                stt(tmpm[:], zA[:, cc, 1:513], acw[:, l, cc, 1:2], tmpm[:], ALU.mult, ALU.add, [f'zA{cc}', 'tmpm', 'const'], ['tmpm'])
                stt(tmpm[:], zA[:, cc, 0:512], acw[:, l, cc, 0:1], tmpm[:], ALU.mult, ALU.add, [f'zA{cc}', 'tmpm', 'const'], ['tmpm'])
                tt('dve', abT[:, cc, :], tmpm[:], ps[bb][:], ALU.mult, ['tmpm', PSK[bb]], [f'ab{cc}'])
            wav, wat = wload(win[:, O_BU:O_BU + 512], 8, 512)
            wgv, wgt = wload(win[:, O_BU + 512:O_BU + 1024], 8, 512)
            dgrot = Rot(range(8))

            def b_proj(cc):
                ba, bg = brot.next(), brot.next()
                for (bk, wv_, wt_) in ((ba, wav, wat), (bg, wgv, wgt)):
                    for c in range(8):
                        mm(ps[bk][:], wv_[:, c, cc * 128:(cc + 1) * 128], uT[:, c, ts0:ts0 + 512], c == 0, c == 7,
                           [wt_, UTK], [PSK[bk]])
                if sub == 0:
                    P.add('dve', lambda e, cc=cc: e.memset(yglu[:, cc, 0:30], 0.0), [f'yg{cc}'], [f'yg{cc}'])
                else:
                    cp('dve', yglu[:, cc, 0:30], ygtail[:, cc, :], [f'ygt{cc}'], [f'yg{cc}'])
                si = sgrot.next()
                act(sg_t[si][:], ps[bg][:], AF.Sigmoid, [PSK[bg]], [f'sg{si}'])
                tt('dve', yglu[:, cc, 30:542], sg_t[si][:], ps[ba][:], ALU.mult, [f'sg{si}', PSK[ba]], [f'yg{cc}'])
                if sub < NQC - 1:
                    cp('pool', ygtail[:, cc, :], yglu[:, cc, 512:542], [f'yg{cc}'], [f'ygt{cc}'])

            def b_diag(cc, k):
                di = dgrot.next()
                ts('pool' if k % 3 == 2 else 'dve', dg_t[di][:], ident_b, bcw[:, l, cc, k:k + 1], 1.0, ALU.mult, ALU.mult,
                   ['const'], [f'dg{di}'])
                return di

            def b_pre(cc):
                return [b_diag(cc, k) for k in range(8)]

            def b_conv(cc, dis):
                bk = brot.next()
                dis = list(dis)
                for k in range(31):
                    di = dis[k]
                    mm(ps[bk][:], dg_t[di][:], yglu[:, cc, k:k + 512], k == 0, k == 30, [f'dg{di}', f'yg{cc}'], [PSK[bk]])
                    if k + 8 < 31:
                        dis.append(b_diag(cc, k + 8))
                act(yB[:, cc, :], ps[bk][:], AF.Identity, [PSK[bk], 'const'], [f'yB{cc}'], bias=bvec[:, l, 0, cc:cc + 1])

            b_proj(0)
            for cc in range(4):
                dis = b_pre(cc)
                if cc + 1 < 4:
                    b_proj(cc + 1)
                b_conv(cc, dis)
            def ln_sq(cc):
                sqb = tmpm if cc % 2 == 0 else macc
                sqk = 'tmpm' if cc % 2 == 0 else 'mu'
                tt('pool', sqb[:], yB[:, cc, :], yB[:, cc, :], ALU.mult, [f'yB{cc}'], [sqk])

            def ln_s2(cc):
                sqb = tmpm if cc % 2 == 0 else macc
                sqk = 'tmpm' if cc % 2 == 0 else 'mu'
                mm(ps[7][:], ones_f, sqb[:], cc == 0, cc == 3, [sqk, 'const'], [PSK[7]])

            ln_sq(0)
            ln_sq(1)
            for cc in range(4):
                mm(ps[6][:], ones_f, yB[:, cc, :], cc == 0, cc == 3, [f'yB{cc}', 'const'], [PSK[6]])
            ln_s2(0)
            ln_s2(1)
            ln_sq(2)
            ln_sq(3)
            ln_s2(2)
            ln_s2(3)
            ts('dve', macc[:], ps[6][:], 1.0 / 512, None, ALU.mult, None, [PSK[6]], ['mu'])
            tt('dve', tmpm[:], macc[:], macc[:], ALU.mult, ['mu'], ['tmpm'])
            stt(tmpm[:], ps[7][:], 1.0 / 512, tmpm[:], ALU.mult, ALU.subtract, [PSK[7], 'tmpm'], ['tmpm'])
            act(lnv_t[:], tmpm[:], AF.Ln, ['tmpm'], ['lnv'], bias=EPS)
            act(rs_t[0][:], lnv_t[:], AF.Exp, ['lnv'], ['rstd0'], scale=-0.5)
            for cc in range(4):
                tt('dve', yB[:, cc, :], yB[:, cc, :], macc[:], ALU.subtract, [f'yB{cc}', 'mu'], [f'yB{cc}'])
                tt('dve', yB[:, cc, :], yB[:, cc, :], rs_t[0][:], ALU.mult, [f'yB{cc}', 'rstd0'], [f'yB{cc}'])
                act(abT[:, 4 + cc, :], yB[:, cc, :], AF.Silu, [f'yB{cc}', 'const'], [f'ab{4 + cc}'],
                    bias=bvec[:, l, 2, cc:cc + 1], scale=bvec[:, l, 1, cc:cc + 1])
            srcs = [lambda cc: abT[:, cc, :], lambda cc: abT[:, 4 + cc, :],
                    lambda cc: cdT[:, cc, ts0:ts0 + 512], lambda cc: cdT[:, 4 + cc, ts0:ts0 + 512]]
            stoks = [lambda cc: f'ab{cc}', lambda cc: f'ab{4 + cc}',
                     lambda cc: f'cd{cc}_{sub}', lambda cc: f'cd{4 + cc}_{sub}']
            onames = ['a_w_out', 'b_w_out', 'c_w_out', 'd_w_out']
            ALLAB = [f'zA{cc}' for cc in range(4)] + [f'yg{cc}' for cc in range(4)] + [f'yB{cc}' for cc in range(4)] + [f'dg{i}' for i in range(8)]
            for og in range(2):
                for bri, br in enumerate([2, 3, 0, 1]):
                    wov, wot = wload(WS[onames[br]][l][:, og * 512:(og + 1) * 512], 4, 512)
                    wglv, wglt = wload(win[:, O_G + br * 1024 + og * 512:O_G + br * 1024 + (og + 1) * 512], 8, 512)
                    for oi in range(4):
                        oc = og * 4 + oi
                        by, bgl = brot.next(), brot.next()
                        for cc in range(4):
                            mm(ps[by][:], wov[:, cc, oi * 128:(oi + 1) * 128], srcs[br](cc), cc == 0, cc == 3,
                               [wot, stoks[br](cc)], [PSK[by]])
                        for c in range(8):
                            mm(ps[bgl][:], wglv[:, c, oi * 128:(oi + 1) * 128], uT[:, c, ts0:ts0 + 512], c == 0, c == 7,
                               [wglt, UTK], [PSK[bgl]])
                        si = sgrot.next()
                        act(sg_t[si][:], ps[bgl][:], AF.Sigmoid, [PSK[bgl]], [f'sg{si}'])
                        if bri == 0:
                            tt('dve', fTm[:, oc, :], sg_t[si][:], ps[by][:], ALU.mult, [f'sg{si}', PSK[by]],
                               [f'fm{oc}'] + ALLAB)
                        else:
                            tt('dve', tmpm[:], sg_t[si][:], ps[by][:], ALU.mult, [f'sg{si}', PSK[by]], ['tmpm'])
                            if bri < 3:
                                tt('pool', fTm[:, oc, :], fTm[:, oc, :], tmpm[:], ALU.add, ['tmpm', f'fm{oc}'], [f'fm{oc}'])
                            else:
                                tt('pool', mrg[:, oc, :], fTm[:, oc, :], tmpm[:], ALU.add, ['tmpm', f'fm{oc}'], [f'mrg{oc}'] + ALLAB)
            pendw = []
            for og in range(2):
                wv_, wt_ = wload(WS['w_o'][l][:, og * 512:(og + 1) * 512], 8, 512)
                for oi in range(4):
                    oc = og * 4 + oi
                    b = brot.next()
                    for c in range(8):
                        mm(ps[b][:], wv_[:, c, oi * 128:(oi + 1) * 128], mrg[:, c, :], c == 0, c == 7,
                           [wt_, f'mrg{c}'], [PSK[b]])
                    for (pi_, poc) in pendw:
                        mm(ps[6][:], ones_b, sq_t[pi_][:], poc == 0, poc == 7, [f'sq{pi_}', 'const'], [PSK[6]])
                    pendw = []
                    cp('act', fTm[:, oc, :], ps[b][:], [PSK[b]], [f'fm{oc}'])
                    i = sqrot.next()
                    tt('pool', sq_t[i][:], fTm[:, oc, :], fTm[:, oc, :], ALU.mult, [f'fm{oc}'], [f'sq{i}'])
                    pendw.append((i, oc))
            for (pi_, poc) in pendw:
                mm(ps[6][:], ones_b, sq_t[pi_][:], poc == 0, poc == 7, [f'sq{pi_}', 'const'], [PSK[6]])
            act(lnv_t[:], ps[6][:], AF.Ln, [PSK[6]], ['lnv'], bias=EPS, scale=1.0 / D)
            act(rs_t[1][:], lnv_t[:], AF.Exp, ['lnv'], ['rstd1'], scale=-0.5)
            for oc in range(8):
                stt(fTm[:, oc, :], fTm[:, oc, :], gains[:, l, 3, oc:oc + 1], rs_t[1][:], ALU.mult, ALU.mult,
                    [f'fm{oc}', 'rstd1', 'const'], [f'fm{oc}'])
                tt('pool', hT[:, oc, ts0:ts0 + 512], hT[:, oc, ts0:ts0 + 512], fTm[:, oc, :], ALU.add,
                   [f'fm{oc}', f'hT{oc}_{sub}'], [f'hT{oc}_{sub}'] + ALLAB)

    xrot = Rot(range(2))
    trot = Rot([0, 1, 2, 3])

    def load_seq(s):
        for tb in range(NB):
            i = xrot.next()
            dma(xs_t[i][:], x_d[s, tb * 128:(tb + 1) * 128, :], [], [f'xs{i}'], f'xs{i}')
            for half in range(2):
                b = trot.next()
                for k in range(4):
                    c = half * 4 + k
                    P.add('pe', lambda e, b=b, k=k, c=c, i=i: e.transpose(ps[b][:, k * 128:(k + 1) * 128],
                                                                          xs_t[i][:, c * 128:(c + 1) * 128], ident),
                          [f'xs{i}', 'const'], [PSK[b]])
                cp('dve' if half == 0 else 'act', hT[:, half * 4:half * 4 + 4, tb * 128:(tb + 1) * 128],
                   ps[b][:].rearrange("p (c t) -> p c t", t=128), [PSK[b]], [f'hT{c}_{tb // 4}' for c in range(half * 4, half * 4 + 4)])

    def store_seq(s):
        for tb in range(NB):
            i = xrot.next()
            for half in range(2):
                b = trot.next()
                for k in range(4):
                    c = half * 4 + k
                    P.add('pe', lambda e, b=b, k=k, c=c, tb=tb: e.transpose(ps[b][:, k * 128:(k + 1) * 128],
                                                                            hT[:, c, tb * 128:(tb + 1) * 128], ident),
                          [f'hT{c}_{tb // 4}', 'const'], [PSK[b]])
                cp('dve' if half == 0 else 'act', xs_t[i][:, half * 512:(half + 1) * 512], ps[b][:], [PSK[b]], [f'xs{i}'])
            dma(out_d[s, tb * 128:(tb + 1) * 128, :], xs_t[i][:], [f'xs{i}'], [f'xs{i}'], f'xs{i}')

    convert_next_group()
    convert_next_group()
    convert_next_group()
    barrier()
    for s in range(NSEQ):
        load_seq(s)
        barrier()
        for l in range(NL):
            ffn(l, 1)
            barrier()
            convert_next_group()
            mixer(l)
            barrier()
            convert_next_group()
            ffn(l, 2)
            if l == NL - 1:
                barrier()
            convert_next_group()
        store_seq(s)
        barrier()

    engs = ['pe', 'act', 'dve', 'pool']
    dma_keys = sorted(P.dma_counts.keys())
    sems = {e: es.enter_context(nc.semaphore(f"sem_{e}")) for e in engs}
    dma_sems = {k: es.enter_context(nc.semaphore(f"dsem_{k}")) for k in dma_keys}
    with nc.allow_non_contiguous_dma(reason="tiny parameter-vector loads at setup"), nc.Block() as block:
        P.emit(nc, block, sems, dma_sems, final_waits=['xs0', 'xs1'])
    es.close()
    return nc, len(P.ops)


def _consts(T):
    NB = T // 128
    cf = np.zeros((128, 4, 128), np.float32)
    cf[:, 0, :] = np.eye(128, dtype=np.float32)
    cf[:, 1, :] = 1.0
    kk = np.arange(128)
    cf[:, 2, :] = (kk[:, None] <= kk[None, :]).astype(np.float32)
    E = np.zeros((128, 128), np.float32)
    for h in range(8):
        for j in range(NB):
            for j2 in range(j):
                E[h * NB + j2, h * NB + j] = 1.0
    cf[:, 3, :] = E
    cb = np.zeros((128, 3, 128), np.float32)
    cb[:, 2, :] = np.eye(128, dtype=np.float32)
    cb[:, 0, :] = 1.0
    cb[:, 1, :] = (kk[None, :] >= kk[:, None]).astype(np.float32)
    cb = cb.astype(ml_dtypes.bfloat16)
    al = np.zeros((4, 2, 4, T), np.float32)
    t = np.arange(T, dtype=np.float64)
    for h in range(4):
        slope = 2.0 ** (-2.0 * (h + 1))
        v = 8.0 * slope * t
        hi = v.astype(np.float32).astype(ml_dtypes.bfloat16).astype(np.float64)
        lo = v - hi
        al[h, 0, 0] = -hi
        al[h, 0, 1] = -lo
        al[h, 0, 2:4] = 1.0
        al[h, 1, 0:2] = 1.0
        al[h, 1, 2] = hi
        al[h, 1, 3] = lo
    al = al.astype(ml_dtypes.bfloat16)
    return cf, cb, al


_CACHE = {}


def _get_nc(T, NSEQ, NL):
    key = (T, NSEQ, NL)
    if key not in _CACHE:
        _CACHE[key] = build(T, NSEQ, NL)[0]
    return _CACHE[key]


def run(inputs, n_cores, NL=None, layer0=0):
    x = np.asarray(inputs['x'], np.float32)
    B, T, _ = x.shape
    L = inputs['w_in'].shape[0] if NL is None else NL
    NSEQ = B // n_cores
    nc = _get_nc(T, NSEQ, L)
    cf, cb, al = _consts(T)
    lamc = np.zeros((L, 2), np.float32)
    for l in range(L):
        li = 0.8 - 0.6 * math.exp(-0.3 * (l + layer0))
        lamc[l] = (li, 1.0 - li)
    common = {'cf32': cf, 'cbf': cb, 'alibi': al, 'lamc': lamc}
    for k, v in inputs.items():
        if k == 'x':
            continue
        common[k] = np.ascontiguousarray(np.asarray(v, np.float32)[layer0:layer0 + L])
    in_maps = []
    for i in range(n_cores):
        m = dict(common)
        m['x'] = np.ascontiguousarray(x[i * NSEQ:(i + 1) * NSEQ])
        in_maps.append(m)
    res = run_bass_kernel_spmd(nc, in_maps, core_ids=list(range(n_cores)))
    return np.concatenate([np.asarray(r['out'], np.float32) for r in res.results], axis=0)


def kernel(**inputs):
    return run(inputs, 8)
```

```python
import math
from contextlib import ExitStack
import numpy as np
import ml_dtypes
import concourse.bass as bass
import concourse.mybir as mybir
from concourse.bass_utils import run_bass_kernel_spmd

F32 = mybir.dt.float32
BF16 = mybir.dt.bfloat16
AF = mybir.ActivationFunctionType
ALU = mybir.AluOpType
AX = mybir.AxisListType

D = 1024
DFF = 2816
NF = DFF // 128
EPS = 1e-6
INW = 9736
O_AB, O_AC, O_AX = 0, 512, 1024
O_BU = 1536
O_CQ, O_CK, O_CV = 2560, 3072, 3584
O_DQ, O_DK, O_DV, O_DF = 4096, 4608, 5120, 5632
O_G = 5640

WNAMES = ['ffn1_w_gate', 'ffn1_w_up', 'ffn1_w_down', 'w_in', 'a_w_out', 'b_w_out', 'c_w_out',
          'd_w_out', 'w_o', 'ffn2_w_gate', 'ffn2_w_up', 'ffn2_w_down']
WSHAPE = {'ffn1_w_gate': (D, DFF), 'ffn1_w_up': (D, DFF), 'ffn1_w_down': (DFF, D), 'w_in': (D, INW),
          'a_w_out': (512, D), 'b_w_out': (512, D), 'c_w_out': (512, D), 'd_w_out': (512, D),
          'w_o': (D, D), 'ffn2_w_gate': (D, DFF), 'ffn2_w_up': (D, DFF), 'ffn2_w_down': (DFF, D)}
VNAMES = ['ffn1_pre_g', 'ffn1_post_g', 'mix_pre_g', 'mix_post_g', 'ffn2_pre_g', 'ffn2_post_g']


class Prog:
    def __init__(self):
        self.ops = []
        self.last_writer = {}
        self.readers = {}
        self.dma_counts = {}
        self.last_eng = {}
        self.last_dma = {}

    def add(self, eng, fn, reads=(), writes=(), dma=None, region=True):
        idx = len(self.ops)
        deps = set()
        if region and 'REGION' in self.last_writer:
            deps.add(self.last_writer['REGION'])
        for t in reads:
            w = self.last_writer.get(t)
            if w is not None:
                deps.add(w)
        for t in writes:
            w = self.last_writer.get(t)
            if w is not None:
                deps.add(w)
            for r in self.readers.get(t, ()):
                deps.add(r)
        deps.discard(idx)
        for t in reads:
            self.readers.setdefault(t, []).append(idx)
        for t in writes:
            self.last_writer[t] = idx
            self.readers[t] = []
        dcount = None
        if dma is not None:
            self.dma_counts[dma] = self.dma_counts.get(dma, 0) + 1
            dcount = self.dma_counts[dma]
        self.ops.append((eng, fn, deps, dma, dcount))
        if dma is None:
            self.last_eng[eng] = idx
        else:
            self.last_dma[dma] = idx
        return idx

    def barrier(self, fn):
        idx = len(self.ops)
        deps = set(self.last_eng.values()) | set(v for k, v in self.last_dma.items() if not k.startswith('cv'))
        self.ops.append(('dve', fn, deps, None, None))
        self.last_eng['dve'] = idx
        self.last_writer['REGION'] = idx
        return idx

    def emit(self, nc, block, sems, dma_sems, final_waits):
        ops = self.ops
        n = len(ops)
        milestone = [False] * n
        for i, (eng, fn, deps, dma, dc) in enumerate(ops):
            for d in deps:
                de = ops[d][0]
                if ops[d][3] is not None:
                    continue
                if de != eng or eng != 'pe':
                    milestone[d] = True
        count = {}
        mcount = [0] * n
        for i, (eng, fn, deps, dma, dc) in enumerate(ops):
            if dma is None and milestone[i]:
                count[eng] = count.get(eng, 0) + 1
                mcount[i] = count[eng]
        per_eng = {}
        for i, o in enumerate(ops):
            per_eng.setdefault(o[0], []).append(i)
        import bisect
        dma_idx = {}
        for i, o in enumerate(ops):
            if o[3] is not None:
                dma_idx.setdefault(o[3], []).append(i)

        def body(engname):
            def f(e):
                waited = {}
                for i in per_eng.get(engname, []):
                    eng, fn, deps, dma, dc = ops[i]
                    need = {}
                    for d in deps:
                        de, _, _, ddma, ddc = ops[d]
                        if ddma is not None:
                            key = ('dma', ddma)
                            val = 16 * bisect.bisect_left(dma_idx[ddma], i)
                        else:
                            if de == eng and eng == 'pe':
                                continue
                            key = ('eng', de)
                            val = mcount[d]
                        if val > need.get(key, 0):
                            need[key] = val
                    for key, val in need.items():
                        if waited.get(key, 0) >= val:
                            continue
                        waited[key] = val
                        s = dma_sems[key[1]] if key[0] == 'dma' else sems[key[1]]
                        e.wait_ge(s, val)
                    inst = fn(e)
                    if dma is not None:
                        inst.then_inc(dma_sems[dma], 16)
                    elif milestone[i]:
                        inst.then_inc(sems[eng], 1)
                if engname == 'sp':
                    for key in final_waits:
                        e.wait_ge(dma_sems[key], 16 * self.dma_counts[key])
            return f

        block.tensor(body('pe'))
        block.scalar(body('act'))
        block.vector(body('dve'))
        block.gpsimd(body('pool'))
        block.sync(body('sp'))


class Rot:
    def __init__(self, items):
        self.items = list(items)
        self.i = 0

    def next(self):
        v = self.items[self.i % len(self.items)]
        self.i += 1
        return v


def build(T, NSEQ, NL):
    NB = T // 128
    NQC = T // 512
    NT = min(1024, T)
    NPASS = T // NT
    NSUBP = NT // 512
    NBH = 8 * NB
    assert T % 512 == 0 and NBH <= 128

    nc = bass.Bass("TRN2", target_bir_lowering=False, dynamic_dma_scratch_size=1024)
    P = Prog()

    x_d = nc.dram_tensor("x", [NSEQ, T, D], F32, kind="ExternalInput").ap()
    out_d = nc.dram_tensor("out", [NSEQ, T, D], F32, kind="ExternalOutput").ap()
    W = {}
    for nm in WNAMES:
        K, N = WSHAPE[nm]
        W[nm] = nc.dram_tensor(nm, [NL, K, N], F32, kind="ExternalInput").ap()
    WS = {}
    for nm in WNAMES:
        K, N = WSHAPE[nm]
        WS[nm] = [nc.dram_tensor(f"{nm}_bf{l}", [K, N], BF16, kind="Internal").ap() for l in range(NL)]
    V = {}
    for nm in VNAMES:
        V[nm] = nc.dram_tensor(nm, [NL, D], F32, kind="ExternalInput").ap()
    a_conv_w = nc.dram_tensor("a_conv_w", [NL, 3, 512], F32, kind="ExternalInput").ap()
    b_conv_w = nc.dram_tensor("b_conv_w", [NL, 31, 512], F32, kind="ExternalInput").ap()
    b_conv_b = nc.dram_tensor("b_conv_b", [NL, 512], F32, kind="ExternalInput").ap()
    b_ln_g = nc.dram_tensor("b_ln_g", [NL, 512], F32, kind="ExternalInput").ap()
    b_ln_b = nc.dram_tensor("b_ln_b", [NL, 512], F32, kind="ExternalInput").ap()
    lamv = {}
    for nm in ['c_lam_q1', 'c_lam_k1', 'c_lam_q2', 'c_lam_k2']:
        lamv[nm] = nc.dram_tensor(nm, [NL, 64], F32, kind="ExternalInput").ap()
    c_subln_g = nc.dram_tensor("c_subln_g", [NL, 128], F32, kind="ExternalInput").ap()
    d_forget_b = nc.dram_tensor("d_forget_b", [NL, 8], F32, kind="ExternalInput").ap()
    cf32_d = nc.dram_tensor("cf32", [128, 4, 128], F32, kind="ExternalInput").ap()
    cbf_d = nc.dram_tensor("cbf", [128, 3, 128], BF16, kind="ExternalInput").ap()
    alibi_d = nc.dram_tensor("alibi", [4, 2, 4, T], BF16, kind="ExternalInput").ap()
    lamc_d = nc.dram_tensor("lamc", [NL, 2], F32, kind="ExternalInput").ap()
    p3_d = nc.dram_tensor("p3_d", [128, 6, 128], BF16, kind="Internal").ap()

    off = [(int(nc.sbuf_base) + 255) // 256 * 256]

    def alloc(name, shape, dt, at=None):
        nbytes = int(np.prod(shape[1:])) * (4 if dt == F32 else 2)
        nbytes = (nbytes + 63) // 64 * 64
        if at is None:
            o = off[0]
            off[0] += nbytes
        else:
            o = at
        assert o + nbytes <= 229344, (name, o, nbytes)
        return nc.alloc_sbuf_tensor_at(name, list(shape), dt, offset=o), o + nbytes

    hT, _ = alloc("hT", [128, 8, T], F32)
    cf32, _ = alloc("cf32s", [128, 4, 128], F32)
    cbf, _ = alloc("cbfs", [128, 3, 128], BF16)
    ident = cf32[:, 0, :]
    ones_f = cf32[:, 1, :]
    U_f = cf32[:, 2, :]
    E_f = cf32[:, 3, :]
    ones_b = cbf[:, 0, :]
    maskT = cbf[:, 1, :]
    ident_b = cbf[:, 2, :]
    gains, _ = alloc("gains", [128, NL, 6, 8], F32)
    ghalf, _ = alloc("ghalf", [128, NL, 2, 8], F32)
    acw, _ = alloc("acw", [128, NL, 4, 3], F32)
    bcw, _ = alloc("bcw", [128, NL, 4, 31], F32)
    bvec, _ = alloc("bvec", [128, NL, 3, 4], F32)
    subg, _ = alloc("subg", [128, NL], F32)
    neglam, _ = alloc("neglam", [128, NL], F32)
    lamt2, _ = alloc("lamt2", [128, NL, 8], F32)
    fb16, _ = alloc("fb16", [128, NL, 128], F32)
    fb8, _ = alloc("fb8", [128, NL, 8], F32)
    dummy, _ = alloc("dummy", [128, 16], F32)
    NSLOT = 4
    wslot = []
    for i in range(NSLOT):
        t_, _ = alloc(f"wslot{i}", [128, 4096], BF16)
        wslot.append(t_)
    wf_t, _ = alloc("wf", [128, 8, 8], BF16)
    rs_t = []
    for i in range(2):
        t_, _ = alloc(f"rstd{i}", [128, 512], F32)
        rs_t.append(t_)
    lnv_t, _ = alloc("lnv", [128, 512], F32)
    sq_t = []
    for i in range(4):
        t_, _ = alloc(f"sq{i}", [128, 512], BF16)
        sq_t.append(t_)
    PH = off[0]
    o = PH
    lamtmp, o = alloc("lamtmp", [128, NL, 4, 64], F32, at=o)
    o = PH
    xn, o = alloc("xn", [128, 8, NT], BF16, at=o)
    actT, o = alloc("actT", [128, NF, NT], BF16, at=o)
    fT, o = alloc("fT", [128, 8, NT], F32, at=o)
    s_t = []
    for i in range(2):
        t_, o = alloc(f"sil{i}", [128, 512], F32, at=o)
        s_t.append(t_)
    tmpx = []
    for i in range(2):
        t_, o = alloc(f"tmpx{i}", [128, 512], F32, at=o)
        tmpx.append(t_)
    xs_t = []
    o2 = PH + 8 * NT * 2
    for i in range(2):
        t_, o2 = alloc(f"xs{i}", [128, 1024], F32, at=o2)
        xs_t.append(t_)
    o = PH
    uT, o = alloc("uT", [128, 8, T], BF16, at=o)
    cdT, o = alloc("cdT", [128, 8, T], BF16, at=o)
    MX = o
    vt, o = alloc("vt", [128, NB, 512], BF16, at=o)
    qa, ka = [], []
    for i in range(2):
        t_, o = alloc(f"qa{i}", [128, T], BF16, at=o)
        qa.append(t_)
        t_, o = alloc(f"ka{i}", [128, T], BF16, at=o)
        ka.append(t_)
    pt_t = []
    for i in range(4):
        t_, o = alloc(f"pt{i}", [128, 512], BF16, at=o)
        pt_t.append(t_)
    ot_t = []
    for i in range(3):
        t_, o = alloc(f"ot{i}", [128, 512], F32, at=o)
        ot_t.append(t_)
    cz, ce, clf, ctb = (ot_t[2][:, 0:128], ot_t[2][:, 128:256], ot_t[2][:, 256:384], ot_t[2][:, 384:512])
    c8, o = alloc("c8", [128, 128], F32, at=o)
    cr1, o = alloc("cr1", [128, 128], F32, at=o)
    cr2, o = alloc("cr2", [128, 128], F32, at=o)
    p3, o = alloc("p3", [128, 6, 128], BF16, at=o)
    o = MX
    zA, o = alloc("zA", [128, 4, 514], F32, at=o)
    yglu, o = alloc("yglu", [128, 4, 542], BF16, at=o)
    yB, o = alloc("yB", [128, 4, 512], F32, at=o)
    dg_t = []
    for i in range(8):
        t_, o = alloc(f"dg{i}", [128, 128], BF16, at=o)
        dg_t.append(t_)
    MX2 = o
    fTm, _ = alloc("fTm", [128, 8, 512], F32, at=MX)
    abT, o = alloc("abT", [128, 8, 512], BF16, at=max(MX2, MX + 8 * 512 * 4 + 8 * 512 * 2))
    mrg, _ = alloc("mrg", [128, 8, 512], BF16, at=MX + 8 * 512 * 4)
    sg_t = []
    for i in range(2):
        t_, o = alloc(f"sg{i}", [128, 512], F32, at=o)
        sg_t.append(t_)
    tmpm, o = alloc("tmpm", [128, 512], F32, at=o)
    macc, o = alloc("macc", [128, 512], F32, at=o)
    zAtail, o = alloc("zAtail", [128, 4, 2], F32, at=o)
    ygtail, o = alloc("ygtail", [128, 4, 30], BF16, at=o)
    print("SBUF persistent end", PH, "mixer end", o)

    es = ExitStack()
    ps = [es.enter_context(nc.psum_tensor(f"ps{i}", [128, 512], F32)) for i in range(8)]
    PSK = [f"ps{i}" for i in range(8)]

    def dma(out, in_, reads, writes, key):
        P.add('sp', lambda e, out=out, in_=in_: e.dma_start(out=out, in_=in_), reads, writes, dma=key)

    def mm(out, lhsT, rhs, start, stop, reads, writes):
        P.add('pe', lambda e, out=out, lhsT=lhsT, rhs=rhs, start=start, stop=stop:
              e.matmul(out, lhsT, rhs, start=start, stop=stop), reads, writes)

    def act(out, in_, func, reads, writes, bias=None, scale=None):
        kw = {}
        if bias is not None:
            kw['bias'] = bias
        if scale is not None:
            kw['scale'] = scale
        P.add('act', lambda e, out=out, in_=in_, func=func, kw=kw: e.activation(out, in_, func, **kw), reads, writes)

    def tt(eng, out, in0, in1, op, reads, writes):
        P.add(eng, lambda e, out=out, in0=in0, in1=in1, op=op: e.tensor_tensor(out, in0, in1, op), reads, writes)

    def ts(eng, out, in0, s1, s2, op0, op1, reads, writes):
        if op1 is None:
            P.add(eng, lambda e, out=out, in0=in0, s1=s1, op0=op0: e.tensor_scalar(out, in0, s1, None, op0), reads, writes)
        else:
            P.add(eng, lambda e, out=out, in0=in0, s1=s1, s2=s2, op0=op0, op1=op1:
                  e.tensor_scalar(out, in0, s1, s2, op0, op1), reads, writes)

    def stt(out, in0, scalar, in1, op0, op1, reads, writes):
        P.add('dve', lambda e, out=out, in0=in0, scalar=scalar, in1=in1, op0=op0, op1=op1:
              e.scalar_tensor_tensor(out, in0, scalar, in1, op0, op1), reads, writes)

    def cp(eng, out, in_, reads, writes):
        if eng == 'act':
            P.add('act', lambda e, out=out, in_=in_: e.copy(out, in_), reads, writes)
        else:
            P.add(eng, lambda e, out=out, in_=in_: e.tensor_copy(out, in_), reads, writes)

    dma(cf32[:], cf32_d, [], [f'cst_{len(P.ops)}'], 'c0')
    dma(cbf[:], cbf_d, [], [f'cst_{len(P.ops)}'], 'c0')
    for l in range(NL):
        for i, nm in enumerate(VNAMES):
            dma(gains[:, l, i, :], V[nm][l].rearrange("(c p) -> p c", p=128), [], [f'cst_{len(P.ops)}'], 'c0')
        for cc in range(4):
            dma(acw[:, l, cc, :], a_conv_w[l][:, cc * 128:(cc + 1) * 128].rearrange("k p -> p k"), [], [f'cst_{len(P.ops)}'], 'c0')
            dma(bcw[:, l, cc, :], b_conv_w[l][:, cc * 128:(cc + 1) * 128].rearrange("k p -> p k"), [], [f'cst_{len(P.ops)}'], 'c0')
        for i, v_ in enumerate([b_conv_b, b_ln_g, b_ln_b]):
            dma(bvec[:, l, i, :], v_[l].rearrange("(c p) -> p c", p=128), [], [f'cst_{len(P.ops)}'], 'c0')
        dma(subg[:, l:l + 1], c_subln_g[l].rearrange("(p o) -> p o", o=1), [], [f'cst_{len(P.ops)}'], 'c0')
        dma(fb8[:, l, :], d_forget_b[l:l + 1, :].broadcast_to([128, 8]), [], [f'cst_{len(P.ops)}'], 'c0')
        for i, nm in enumerate(['c_lam_q1', 'c_lam_k1', 'c_lam_q2', 'c_lam_k2']):
            dma(lamtmp[:, l, i, :], lamv[nm][l:l + 1, :].broadcast_to([128, 64]), [], [f'cst_{len(P.ops)}'], 'c0')
        dma(lamt2[:, l, 6:8], lamc_d[l:l + 1, :].broadcast_to([128, 2]), [], [f'cst_{len(P.ops)}'], 'c0')

    NSETUP = len(P.ops)
    P.add('dve', lambda e: e.memset(dummy[:], 0.0), [f'cst_{i}' for i in range(NSETUP)], ['const'])
    for l in range(NL):
        cp('dve', fb16[:, l, :].rearrange("p (h j) -> p h j", j=16),
           fb8[:, l, :].unsqueeze(2).broadcast_to([128, 8, 16]), ['const'], ['const2'])
    for l in range(NL):
        ts('dve', ghalf[:, l, 0, :], gains[:, l, 1, :], 0.5, None, ALU.mult, None, ['const'], ['const2'])
        ts('dve', ghalf[:, l, 1, :], gains[:, l, 5, :], 0.5, None, ALU.mult, None, ['const'], ['const2'])
        lt, l2 = lamtmp[:, l], lamt2[:, l]
        tt('dve', lt[:, 0, :], lt[:, 0, :], lt[:, 1, :], ALU.mult, ['const'], [f'lamtmp0_{l}'])
        tt('dve', lt[:, 2, :], lt[:, 2, :], lt[:, 3, :], ALU.mult, ['const'], [f'lamtmp2_{l}'])
        P.add('dve', lambda e, lt=lt, l2=l2: e.reduce_sum(l2[:, 0:1], lt[:, 0, :], AX.X), [f'lamtmp0_{l}'], [f'lt0_{l}'])
        P.add('dve', lambda e, lt=lt, l2=l2: e.reduce_sum(l2[:, 1:2], lt[:, 2, :], AX.X), [f'lamtmp2_{l}'], [f'lt1_{l}'])
        act(l2[:, 2:4], l2[:, 0:2], AF.Exp, [f'lt0_{l}', f'lt1_{l}'], [f'lt2_{l}'])
        tt('dve', l2[:, 4:5], l2[:, 3:4], l2[:, 2:3], ALU.subtract, [f'lt2_{l}'], [f'lt4_{l}'])
        tt('dve', neglam[:, l:l + 1], l2[:, 4:5], l2[:, 6:7], ALU.subtract, [f'lt4_{l}', 'const'], ['const2'])
        tt('dve', subg[:, l:l + 1], subg[:, l:l + 1], l2[:, 7:8], ALU.mult, ['const'], ['const2'])

    CONVTOK = {}

    def convert(nm, l):
        K, N = WSHAPE[nm]
        key = f'cv{l}_{nm}'
        toks = []
        for c in range(K // 128):
            for n0 in range(0, N, 2048):
                n1 = min(N, n0 + 2048)
                tok = f'wscr{len(P.ops)}'
                P.add('pool', lambda e, o_=WS[nm][l][c * 128:(c + 1) * 128, n0:n1], i_=W[nm][l, c * 128:(c + 1) * 128, n0:n1]:
                      e.dma_start(out=o_, in_=i_), [], [tok], dma=key, region=False)
                toks.append(tok)
        CONVTOK[f"{nm}_bf{l}"] = toks

    CGROUPS = []
    for l in range(NL):
        CGROUPS.append([('ffn1_w_gate', l), ('ffn1_w_up', l), ('ffn1_w_down', l)])
        CGROUPS.append([('w_in', l), ('a_w_out', l), ('b_w_out', l), ('c_w_out', l), ('d_w_out', l), ('w_o', l)])
        CGROUPS.append([('ffn2_w_gate', l), ('ffn2_w_up', l), ('ffn2_w_down', l)])
    cg_next = [0]

    def convert_next_group():
        if cg_next[0] < len(CGROUPS):
            for (nm, l) in CGROUPS[cg_next[0]]:
                convert(nm, l)
            cg_next[0] += 1

    streamed = set()

    srot = Rot(range(NSLOT))

    def wload(src_ap, kc, ncols):
        i = srot.next()
        view = wslot[i][:, 0:kc * ncols].rearrange("p (c n) -> p c n", n=ncols)
        reads = []
        nm = src_ap.name
        if nm not in streamed:
            reads = CONVTOK[nm]
            streamed.add(nm)
        P.add('sp', lambda e, out=view, in_=src_ap.rearrange("(c p) n -> p c n", p=128): e.dma_start(out=out, in_=in_),
              reads, [f'wslot{i}'], dma=f'ws{i}', region=False)
        return view, f'wslot{i}'

    sqrot = Rot(range(4))

    def norm_stats(src_fn, src_tok_fn, nchunks, t0, ss_bank, denom, rs_idx):
        for c in range(nchunks):
            i = sqrot.next()
            if c % 2 == 0:
                tt('pool', sq_t[i][:], src_fn(c), src_fn(c), ALU.mult, src_tok_fn(c), [f'sq{i}'])
            else:
                act(sq_t[i][:], src_fn(c), AF.Square, src_tok_fn(c), [f'sq{i}'])
            mm(ps[ss_bank][:], ones_b, sq_t[i][:], c == 0, c == nchunks - 1, [f'sq{i}', 'const'], [PSK[ss_bank]])
        act(lnv_t[:], ps[ss_bank][:], AF.Ln, [PSK[ss_bank]], ['lnv'], bias=EPS, scale=1.0 / denom)
        act(rs_t[rs_idx][:], lnv_t[:], AF.Exp, ['lnv'], [f'rstd{rs_idx}'], scale=-0.5)

    def prenorm_stats(ts0, bank):
        tsub = ts0 // 512
        for c in range(8):
            i = sqrot.next()
            src = hT[:, c, ts0:ts0 + 512]
            if c % 2 == 0:
                tt('pool', sq_t[i][:], src, src, ALU.mult, [f'hT{c}_{tsub}'], [f'sq{i}'])
            else:
                act(sq_t[i][:], src, AF.Square, [f'hT{c}_{tsub}'], [f'sq{i}'])
            mm(ps[bank][:], ones_b, sq_t[i][:], c == 0, c == 7, [f'sq{i}', 'const'], [PSK[bank]])

    def prenorm_scale(l, gi, dst_fn, dst_tok_fn, ts0, bank, ri, tmps):
        tsub = ts0 // 512
        act(lnv_t[:], ps[bank][:], AF.Ln, [PSK[bank]], ['lnv'], bias=EPS, scale=1.0 / D)
        act(rs_t[ri][:], lnv_t[:], AF.Exp, ['lnv'], [f'rstd{ri}'], scale=-0.5)
        for c in range(8):
            if c % 2 == 0:
                stt(dst_fn(c), hT[:, c, ts0:ts0 + 512], gains[:, l, gi, c:c + 1], rs_t[ri][:], ALU.mult, ALU.mult,
                    [f'hT{c}_{tsub}', f'rstd{ri}', 'const'], [dst_tok_fn(c)])
            else:
                tmp, ttok = tmps[(c // 2) % 2]
                tt('pool', tmp[:], hT[:, c, ts0:ts0 + 512], rs_t[ri][:], ALU.mult, [f'hT{c}_{tsub}', f'rstd{ri}'], [ttok])
                act(dst_fn(c), tmp[:], AF.Identity, [ttok, 'const'], [dst_tok_fn(c)], scale=gains[:, l, gi, c:c + 1])

    def ffn(l, which):
        gi_pre, hi = (0, 0) if which == 1 else (4, 1)
        pfx = 'ffn1' if which == 1 else 'ffn2'
        wg, wu, wd = WS[pfx + '_w_gate'][l], WS[pfx + '_w_up'][l], WS[pfx + '_w_down'][l]
        brot = Rot(range(6))
        srt = Rot(range(2))
        tmps = [(tmpx[0], 'tmpx0'), (tmpx[1], 'tmpx1')]

        def pre(pss):
            banks = [brot.next() for _ in range(NSUBP)]
            for sub in range(NSUBP):
                prenorm_stats(pss * NT + sub * 512, banks[sub])
            for sub in range(NSUBP):
                prenorm_scale(l, gi_pre, lambda c, sub=sub: xn[:, c, sub * 512:(sub + 1) * 512],
                              lambda c, sub=sub: f'xn{c}_{sub}', pss * NT + sub * 512, banks[sub], sub % 2, tmps)

        pre(0)
        for pss in range(NPASS):
            t0 = pss * NT
            for _ in range((2 + NPASS - 1) // NPASS):
                convert_next_group()
            for fg in range(0, NF, 4):
                nfg = min(4, NF - fg)
                wgv, wgt = wload(wg[:, fg * 128:(fg + nfg) * 128], 8, nfg * 128)
                wuv, wut = wload(wu[:, fg * 128:(fg + nfg) * 128], 8, nfg * 128)
                for sub in range(NSUBP):
                    for fi in range(nfg):
                        f = fg + fi
                        bg, bu = brot.next(), brot.next()
                        for c in range(8):
                            mm(ps[bg][:], wgv[:, c, fi * 128:(fi + 1) * 128], xn[:, c, sub * 512:(sub + 1) * 512],
                               c == 0, c == 7, [wgt, f'xn{c}_{sub}'], [PSK[bg]])
                        for c in range(8):
                            mm(ps[bu][:], wuv[:, c, fi * 128:(fi + 1) * 128], xn[:, c, sub * 512:(sub + 1) * 512],
                               c == 0, c == 7, [wut, f'xn{c}_{sub}'], [PSK[bu]])
                        si = srt.next()
                        act(s_t[si][:], ps[bg][:], AF.Silu, [PSK[bg]], [f'sil{si}'])
                        tt('dve', actT[:, f, sub * 512:(sub + 1) * 512], s_t[si][:], ps[bu][:], ALU.mult,
                           [f'sil{si}', PSK[bu]], [f'act{f}_{sub}'])
            if pss + 1 < NPASS:
                pre(pss + 1)
            pend = []
            for oc in range(8):
                wdv, wdt = wload(wd[:, oc * 128:(oc + 1) * 128], NF, 128)
                for sub in range(NSUBP):
                    b = brot.next()
                    for f in range(NF):
                        mm(ps[b][:], wdv[:, f, :], actT[:, f, sub * 512:(sub + 1) * 512], f == 0, f == NF - 1,
                           [wdt, f'act{f}_{sub}'], [PSK[b]])
                    for (pi_, psub, poc) in pend:
                        mm(ps[6 + psub][:], ones_b, sq_t[pi_][:], poc == 0, poc == 7, [f'sq{pi_}', 'const'], [PSK[6 + psub]])
                    pend = []
                    cp('act', fT[:, oc, sub * 512:(sub + 1) * 512], ps[b][:], [PSK[b]], [f'fT{oc}_{sub}'])
                    i = sqrot.next()
                    tt('pool', sq_t[i][:], fT[:, oc, sub * 512:(sub + 1) * 512], fT[:, oc, sub * 512:(sub + 1) * 512],
                       ALU.mult, [f'fT{oc}_{sub}'], [f'sq{i}'])
                    pend.append((i, sub, oc))
            for (pi_, psub, poc) in pend:
                mm(ps[6 + psub][:], ones_b, sq_t[pi_][:], poc == 0, poc == 7, [f'sq{pi_}', 'const'], [PSK[6 + psub]])
            for sub in range(NSUBP):
                ts0 = t0 + sub * 512
                tsub = ts0 // 512
                act(lnv_t[:], ps[6 + sub][:], AF.Ln, [PSK[6 + sub]], ['lnv'], bias=EPS, scale=1.0 / D)
                act(rs_t[sub % 2][:], lnv_t[:], AF.Exp, ['lnv'], [f'rstd{sub % 2}'], scale=-0.5)
                for oc in range(8):
                    stt(fT[:, oc, sub * 512:(sub + 1) * 512], fT[:, oc, sub * 512:(sub + 1) * 512],
                        ghalf[:, l, hi, oc:oc + 1], rs_t[sub % 2][:], ALU.mult, ALU.mult,
                        [f'fT{oc}_{sub}', f'rstd{sub % 2}', 'const2'], [f'fT{oc}_{sub}'])
                    tt('pool', hT[:, oc, ts0:ts0 + 512], hT[:, oc, ts0:ts0 + 512], fT[:, oc, sub * 512:(sub + 1) * 512],
                       ALU.add, [f'fT{oc}_{sub}', f'hT{oc}_{tsub}'], [f'hT{oc}_{tsub}'])

    def barrier():
        P.barrier(lambda e: e.memset(dummy[:], 0.0))

    def mixer(l):
        win = WS['w_in'][l]
        for qc in range(NQC):
            prenorm_stats(qc * 512, 4 + qc % 4)
        for qc in range(NQC):
            prenorm_scale(l, 2, lambda c, qc=qc: uT[:, c, qc * 512:(qc + 1) * 512], lambda c, qc=qc: f'uT{qc}',
                          qc * 512, 4 + qc % 4, qc % 2, [(ot_t[0], 'o1'), (ot_t[1], 'o2')])
        strot = Rot([0, 1, 2, 3])
        prot = strot
        ptrot = Rot(range(4))
        odrot = Rot([(4, 5), (6, 7)])

        def proj_fm(wv, wt, col0, ncol, dst_fn, dst_tok):
            for qc in range(NQC):
                b = prot.next()
                for c in range(8):
                    mm(ps[b][0:ncol, :], wv[:, c, col0:col0 + ncol], uT[:, c, qc * 512:(qc + 1) * 512],
                       c == 0, c == 7, [wt, f'uT{qc}'], [PSK[b]])
                cp('dve', dst_fn(qc), ps[b][0:ncol, :], [PSK[b]], [dst_tok])

        def v_tokmajor(wvv, wvt):
            for j in range(NB):
                b = prot.next()
                for c in range(8):
                    mm(ps[b][:], uT[:, c, j * 128:(j + 1) * 128], wvv[:, c, :], c == 0, c == 7,
                       [wvt, f'uT{j // 4}'], [PSK[b]])
                cp('dve', vt[:, j, :], ps[b][:], [PSK[b]], ['vt'])

        def attention_qc(qt, kt, nk, v_fn, o_rows, fin_fn, qtok, ktok, qc):
            r0, r1 = o_rows
            bo, bd = odrot.next()
            njb = 4 * qc + 4
            LOOK = 3

            def geom(j):
                qlo = max(qc * 512, j * 128)
                return qlo, (qc + 1) * 512 - qlo, qlo - qc * 512

            def issue_qk(j):
                qlo, w, c0 = geom(j)
                sb = strot.next()
                mm(ps[sb][:, 0:w], kt[0:nk, j * 128:(j + 1) * 128], qt[0:nk, qlo:qlo + w], True, True,
                   [qtok, ktok], [PSK[sb]])
                pi = ptrot.next()
                act(pt_t[pi][:, 0:w], ps[sb][:, 0:w], AF.Exp, [PSK[sb]], [f'pt{pi}'], scale=0.125)
                if j >= 4 * qc:
                    tt('pool', pt_t[pi][:, 0:128], pt_t[pi][:, 0:128], maskT, ALU.mult, [f'pt{pi}', 'const'], [f'pt{pi}'])
                return pi

            def issue_pv(j, pi):
                qlo, w, c0 = geom(j)
                mm(ps[bo][r0:r1, c0:512], v_fn(j), pt_t[pi][:, 0:w], j == 0, j == njb - 1, ['vt', f'pt{pi}'], [PSK[bo]])
                mm(ps[bd][:, c0:512], ones_b, pt_t[pi][:, 0:w], j == 0, j == njb - 1, ['const', f'pt{pi}'], [PSK[bd]])

            pis = []
            for j in range(njb):
                pis.append(issue_qk(j))
                if j >= LOOK:
                    issue_pv(j - LOOK, pis[j - LOOK])
            for j in range(max(0, njb - LOOK), njb):
                issue_pv(j, pis[j])
            fin_fn(qc, bo, bd)

        def proj_fm2(wv, wt, col0, dst0_fn, dst1_fn, tok0, tok1):
            for qc in range(NQC):
                b = prot.next()
                for c in range(8):
                    mm(ps[b][:], wv[:, c, col0:col0 + 128], uT[:, c, qc * 512:(qc + 1) * 512],
                       c == 0, c == 7, [wt, f'uT{qc}'], [PSK[b]])
                cp('dve', dst0_fn(qc), ps[b][0:64, :], [PSK[b]], [tok0])
                cp('act', dst1_fn(qc), ps[b][64:128, :], [PSK[b]], [tok1])

        cdefer = []
        wvv, wvt = wload(win[:, O_CV:O_CV + 512], 8, 512)
        wqv, wqt = wload(win[:, O_CQ:O_CQ + 512], 8, 512)
        wkv, wkt = wload(win[:, O_CK:O_CK + 512], 8, 512)
        v_tokmajor(wvv, wvt)
        for hc in range(4):
            for m in range(2):
                dma(qa[m][64:68, :], alibi_d[hc, 0], [], [f'qa{m}'], f'qa{m}')
                dma(ka[m][64:68, :], alibi_d[hc, 1], [], [f'ka{m}'], f'ka{m}')
            proj_fm2(wqv, wqt, hc * 128, lambda qc: qa[0][0:64, qc * 512:(qc + 1) * 512],
                     lambda qc: qa[1][0:64, qc * 512:(qc + 1) * 512], 'qa0', 'qa1')
            proj_fm2(wkv, wkt, hc * 128, lambda qc: ka[0][0:64, qc * 512:(qc + 1) * 512],
                     lambda qc: ka[1][0:64, qc * 512:(qc + 1) * 512], 'ka0', 'ka1')
            for qc in range(NQC):
                ri = qc % 2

                def fin0(qc, bo, bd, ri=ri):
                    while cdefer:
                        cdefer.pop(0)()
                    act(lnv_t[:], ps[bd][:], AF.Ln, [PSK[bd]], ['lnv'])
                    act(rs_t[ri][:], lnv_t[:], AF.Exp, ['lnv'], [f'rstd{ri}'], scale=-1.0)
                    tt('dve', ot_t[0][:], ps[bo][:], rs_t[ri][:], ALU.mult, [PSK[bo], f'rstd{ri}'], ['o1'])

                def fin1(qc, bo, bd, ri=ri, hc=hc):
                    act(lnv_t[:], ps[bd][:], AF.Ln, [PSK[bd]], ['lnv'])
                    act(rs_t[ri][:], lnv_t[:], AF.Exp, ['lnv'], [f'rstd{ri}'], scale=-1.0)
                    tt('dve', ot_t[1][:], ps[bo][:], rs_t[ri][:], ALU.mult, [PSK[bo], f'rstd{ri}'], ['o2'])
                    stt(ot_t[2][:], ot_t[1][:], neglam[:, l:l + 1], ot_t[0][:], ALU.mult, ALU.add,
                        ['o2', 'o1', 'const2'], ['od'])
                    i = sqrot.next()
                    tt('pool', sq_t[i][:], ot_t[2][:], ot_t[2][:], ALU.mult, ['od'], [f'sq{i}'])

                    def later(i=i, ri=ri, hc=hc, qc=qc):
                        b = prot.next()
                        mm(ps[b][:], ones_b, sq_t[i][:], True, True, [f'sq{i}', 'const'], [PSK[b]])
                        act(lnv_t[:], ps[b][:], AF.Ln, [PSK[b]], ['lnv'], bias=EPS, scale=1.0 / 128)
                        act(rs_t[ri][:], lnv_t[:], AF.Exp, ['lnv'], [f'rstd{ri}'], scale=-0.5)
                        stt(cdT[:, hc, qc * 512:(qc + 1) * 512], ot_t[2][:], subg[:, l:l + 1], rs_t[ri][:],
                            ALU.mult, ALU.mult, ['od', f'rstd{ri}', 'const2'], [f'cd{hc}_{qc}'])
                    cdefer.append(later)

                vf = lambda j, hc=hc: vt[:, j, hc * 128:(hc + 1) * 128]
                attention_qc(qa[0], ka[0], 68, vf, (0, 128), fin0, 'qa0', 'ka0', qc)
                attention_qc(qa[1], ka[1], 68, vf, (0, 128), fin1, 'qa1', 'ka1', qc)

        while cdefer:
            cdefer.pop(0)()
        wvv, wvt = wload(win[:, O_DV:O_DV + 512], 8, 512)
        wqv, wqt = wload(win[:, O_DQ:O_DQ + 512], 8, 512)
        wkv, wkt = wload(win[:, O_DK:O_DK + 512], 8, 512)
        P.add('sp', lambda e: e.dma_start(out=wf_t[:], in_=win[:, O_DF:O_DF + 8].rearrange("(c p) n -> p c n", p=128)),
              [], ['wf'], dma='wf', region=False)
        v_tokmajor(wvv, wvt)
        b = prot.next()
        for j in range(NB):
            for c in range(8):
                mm(ps[b][:, j * 8:(j + 1) * 8], uT[:, c, j * 128:(j + 1) * 128], wf_t[:, c, :], c == 0, c == 7,
                   ['wf', f'uT{j // 4}'], [PSK[b]])
        if NBH < 128:
            P.add('dve', lambda e: e.memset(cz, 30.0), [], ['cz', 'od'])
        tt('dve', cz[:, 0:NBH].rearrange("p (h j) -> p h j", j=NB),
           ps[b][:, 0:NBH].rearrange("p (j h) -> p h j", h=8),
           fb16[:, l, :].rearrange("p (h j) -> p h j", j=16)[:, :, 0:NB], ALU.add, [PSK[b], 'const2'], ['cz', 'od'])
        act(ce, cz, AF.Exp, ['cz'], ['ce'], scale=-1.0)
        act(clf, ce, AF.Ln, ['ce'], ['clf'], bias=1.0)
        b1 = prot.next()
        mm(ps[b1][:, 0:128], clf, ones_f, True, True, ['clf', 'const'], [PSK[b1]])
        cp('dve', ctb, ps[b1][:, 0:128], [PSK[b1]], ['ctb'])
        b2 = prot.next()
        mm(ps[b2][:, 0:128], clf, U_f, True, False, ['clf', 'const'], [PSK[b2]])
        mm(ps[b2][:, 0:128], E_f, ctb, False, True, ['ctb', 'const'], [PSK[b2]])
        ts('dve', c8[:], ps[b2][:, 0:128], 8.0, None, ALU.mult, None, [PSK[b2]], ['c8'])
        cp('dve', p3[:, 0, :], c8[:], ['c8'], ['p3a'])
        tt('dve', cr1[:], c8[:], p3[:, 0, :], ALU.subtract, ['c8', 'p3a'], ['cr1'])
        cp('dve', p3[:, 1, :], cr1[:], ['cr1'], ['p3b'])
        tt('dve', cr2[:], cr1[:], p3[:, 1, :], ALU.subtract, ['cr1', 'p3b'], ['cr2'])
        cp('dve', p3[:, 2, :], cr2[:], ['cr2'], ['p3c'])
        ts('dve', p3[:, 3:6, :], p3[:, 0:3, :], -1.0, None, ALU.mult, None, ['p3a', 'p3b', 'p3c'], ['p3n'])
        dma(p3_d[0:NBH], p3[0:NBH], ['p3a', 'p3b', 'p3c', 'p3n'], ['p3d'], 'p3d')
        P3TOK = ['p3d']
        for i in range(2):
            P.add('dve', lambda e, i=i: e.memset(qa[i][64:70, :], 1.0), [f'qa{i}'], [f'qa{i}'])
            P.add('dve', lambda e, i=i: e.memset(ka[i][64:70, :], 1.0), [f'ka{i}'], [f'ka{i}'])
        for hd in range(8):
            qi = hd % 2
            if hd % 2 == 0:
                dma(qa[0][64:67, :].rearrange("r (j k) -> r j k", k=128),
                    p3_d[hd * NB:(hd + 1) * NB, 3:6, :].rearrange("j r k -> r j k"), P3TOK, ['qa0'], 'qa0')
                dma(ka[0][67:70, :].rearrange("r (j k) -> r j k", k=128),
                    p3_d[hd * NB:(hd + 1) * NB, 0:3, :].rearrange("j r k -> r j k"), P3TOK, ['ka0'], 'ka0')
            if hd % 2 == 0:
                dma(qa[1][64:67, :].rearrange("r (j k) -> r j k", k=128),
                    p3_d[(hd + 1) * NB:(hd + 2) * NB, 3:6, :].rearrange("j r k -> r j k"), P3TOK, ['qa1'], 'qa1')
                dma(ka[1][67:70, :].rearrange("r (j k) -> r j k", k=128),
                    p3_d[(hd + 1) * NB:(hd + 2) * NB, 0:3, :].rearrange("j r k -> r j k"), P3TOK, ['ka1'], 'ka1')
                proj_fm2(wqv, wqt, hd * 64, lambda qc: qa[0][0:64, qc * 512:(qc + 1) * 512],
                         lambda qc: qa[1][0:64, qc * 512:(qc + 1) * 512], 'qa0', 'qa1')
                proj_fm2(wkv, wkt, hd * 64, lambda qc: ka[0][0:64, qc * 512:(qc + 1) * 512],
                         lambda qc: ka[1][0:64, qc * 512:(qc + 1) * 512], 'ka0', 'ka1')
            r0 = (hd % 2) * 64

            def find(qc, bo, bd, hd=hd, r0=r0):
                ri = qc % 2
                act(lnv_t[:], ps[bd][:], AF.Ln, [PSK[bd]], ['lnv'])
                act(rs_t[ri][:], lnv_t[:], AF.Exp, ['lnv'], [f'rstd{ri}'], scale=-1.0)
                tt('dve', cdT[r0:r0 + 64, 4 + hd // 2, qc * 512:(qc + 1) * 512], ps[bo][r0:r0 + 64, :],
                   rs_t[ri][r0:r0 + 64, :], ALU.mult, [PSK[bo], f'rstd{ri}'], [f'cd{4 + hd // 2}_{qc}'])

            for qc in range(NQC):
                attention_qc(qa[qi], ka[qi], 70, lambda j, hd=hd: vt[:, j, (hd // 2) * 128:(hd // 2) * 128 + 128], (0, 128),
                             find, f'qa{qi}', f'ka{qi}', qc)

        barrier()
        brot = Rot(range(6))
        sgrot = Rot(range(2))
        for sub in range(NQC):
            ts0 = sub * 512
            UTK = f'uT{sub}'
            wbv, wbt = wload(win[:, O_AB:O_AB + 512], 8, 512)
            wcv, wct = wload(win[:, O_AC:O_AC + 512], 8, 512)
            wxv, wxt = wload(win[:, O_AX:O_AX + 512], 8, 512)
            for cc in range(4):
                bb, bc_, bx = brot.next(), brot.next(), brot.next()
                for (bk, wv_, wt_) in ((bb, wbv, wbt), (bc_, wcv, wct), (bx, wxv, wxt)):
                    for c in range(8):
                        mm(ps[bk][:], wv_[:, c, cc * 128:(cc + 1) * 128], uT[:, c, ts0:ts0 + 512], c == 0, c == 7,
                           [wt_, UTK], [PSK[bk]])
                if sub == 0:
                    P.add('dve', lambda e, cc=cc: e.memset(zA[:, cc, 0:2], 0.0), [f'zA{cc}'], [f'zA{cc}'])
                else:
                    cp('dve', zA[:, cc, 0:2], zAtail[:, cc, :], [f'zAt{cc}'], [f'zA{cc}'])
                cp('act', sg_t[0][:], ps[bc_][:], [PSK[bc_]], ['sg0'])
                tt('dve', zA[:, cc, 2:514], sg_t[0][:], ps[bx][:], ALU.mult, ['sg0', PSK[bx]], [f'zA{cc}'])
                if sub < NQC - 1:
                    cp('pool', zAtail[:, cc, :], zA[:, cc, 512:514], [f'zA{cc}'], [f'zAt{cc}'])
                ts('dve', tmpm[:], zA[:, cc, 2:514], acw[:, l, cc, 2:3], None, ALU.mult, None, [f'zA{cc}', 'const'], ['tmpm'])
                stt(tmpm[:], zA[:, cc, 1:513], acw[:, l, cc, 1:2], tmpm[:], ALU.mult, ALU.add, [f'zA{cc}', 'tmpm', 'const'], ['tmpm'])
                stt(tmpm[:], zA[:, cc, 0:512], acw[:, l, cc, 0:1], tmpm[:], ALU.mult, ALU.add, [f'zA{cc}', 'tmpm', 'const'], ['tmpm'])
                tt('dve', abT[:, cc, :], tmpm[:], ps[bb][:], ALU.mult, ['tmpm', PSK[bb]], [f'ab{cc}'])
            wav, wat = wload(win[:, O_BU:O_BU + 512], 8, 512)
            wgv, wgt = wload(win[:, O_BU + 512:O_BU + 1024], 8, 512)
            dgrot = Rot(range(8))

            def b_proj(cc):
                ba, bg = brot.next(), brot.next()
                for (bk, wv_, wt_) in ((ba, wav, wat), (bg, wgv, wgt)):
                    for c in range(8):
                        mm(ps[bk][:], wv_[:, c, cc * 128:(cc + 1) * 128], uT[:, c, ts0:ts0 + 512], c == 0, c == 7,
                           [wt_, UTK], [PSK[bk]])
                if sub == 0:
                    P.add('dve', lambda e, cc=cc: e.memset(yglu[:, cc, 0:30], 0.0), [f'yg{cc}'], [f'yg{cc}'])
                else:
                    cp('dve', yglu[:, cc, 0:30], ygtail[:, cc, :], [f'ygt{cc}'], [f'yg{cc}'])
                si = sgrot.next()
                act(sg_t[si][:], ps[bg][:], AF.Sigmoid, [PSK[bg]], [f'sg{si}'])
                tt('dve', yglu[:, cc, 30:542], sg_t[si][:], ps[ba][:], ALU.mult, [f'sg{si}', PSK[ba]], [f'yg{cc}'])
                if sub < NQC - 1:
                    cp('pool', ygtail[:, cc, :], yglu[:, cc, 512:542], [f'yg{cc}'], [f'ygt{cc}'])

            def b_diag(cc, k):
                di = dgrot.next()
                ts('pool' if k % 3 == 2 else 'dve', dg_t[di][:], ident_b, bcw[:, l, cc, k:k + 1], 1.0, ALU.mult, ALU.mult,
                   ['const'], [f'dg{di}'])
                return di

            def b_pre(cc):
                return [b_diag(cc, k) for k in range(8)]

            def b_conv(cc, dis):
                bk = brot.next()
                dis = list(dis)
                for k in range(31):
                    di = dis[k]
                    mm(ps[bk][:], dg_t[di][:], yglu[:, cc, k:k + 512], k == 0, k == 30, [f'dg{di}', f'yg{cc}'], [PSK[bk]])
                    if k + 8 < 31:
                        dis.append(b_diag(cc, k + 8))
                act(yB[:, cc, :], ps[bk][:], AF.Identity, [PSK[bk], 'const'], [f'yB{cc}'], bias=bvec[:, l, 0, cc:cc + 1])

            b_proj(0)
            for cc in range(4):
                dis = b_pre(cc)
                if cc + 1 < 4:
                    b_proj(cc + 1)
                b_conv(cc, dis)
            def ln_sq(cc):
                sqb = tmpm if cc % 2 == 0 else macc
                sqk = 'tmpm' if cc % 2 == 0 else 'mu'
                tt('pool', sqb[:], yB[:, cc, :], yB[:, cc, :], ALU.mult, [f'yB{cc}'], [sqk])

            def ln_s2(cc):
                sqb = tmpm if cc % 2 == 0 else macc
                sqk = 'tmpm' if cc % 2 == 0 else 'mu'
                mm(ps[7][:], ones_f, sqb[:], cc == 0, cc == 3, [sqk, 'const'], [PSK[7]])

            ln_sq(0)
            ln_sq(1)
            for cc in range(4):
                mm(ps[6][:], ones_f, yB[:, cc, :], cc == 0, cc == 3, [f'yB{cc}', 'const'], [PSK[6]])
            ln_s2(0)
            ln_s2(1)
            ln_sq(2)
            ln_sq(3)
            ln_s2(2)
            ln_s2(3)
            ts('dve', macc[:], ps[6][:], 1.0 / 512, None, ALU.mult, None, [PSK[6]], ['mu'])
            tt('dve', tmpm[:], macc[:], macc[:], ALU.mult, ['mu'], ['tmpm'])
            stt(tmpm[:], ps[7][:], 1.0 / 512, tmpm[:], ALU.mult, ALU.subtract, [PSK[7], 'tmpm'], ['tmpm'])
            act(lnv_t[:], tmpm[:], AF.Ln, ['tmpm'], ['lnv'], bias=EPS)
            act(rs_t[0][:], lnv_t[:], AF.Exp, ['lnv'], ['rstd0'], scale=-0.5)
            for cc in range(4):
                tt('dve', yB[:, cc, :], yB[:, cc, :], macc[:], ALU.subtract, [f'yB{cc}', 'mu'], [f'yB{cc}'])
                tt('dve', yB[:, cc, :], yB[:, cc, :], rs_t[0][:], ALU.mult, [f'yB{cc}', 'rstd0'], [f'yB{cc}'])
                act(abT[:, 4 + cc, :], yB[:, cc, :], AF.Silu, [f'yB{cc}', 'const'], [f'ab{4 + cc}'],
                    bias=bvec[:, l, 2, cc:cc + 1], scale=bvec[:, l, 1, cc:cc + 1])
            srcs = [lambda cc: abT[:, cc, :], lambda cc: abT[:, 4 + cc, :],
                    lambda cc: cdT[:, cc, ts0:ts0 + 512], lambda cc: cdT[:, 4 + cc, ts0:ts0 + 512]]
            stoks = [lambda cc: f'ab{cc}', lambda cc: f'ab{4 + cc}',
                     lambda cc: f'cd{cc}_{sub}', lambda cc: f'cd{4 + cc}_{sub}']
            onames = ['a_w_out', 'b_w_out', 'c_w_out', 'd_w_out']
            ALLAB = [f'zA{cc}' for cc in range(4)] + [f'yg{cc}' for cc in range(4)] + [f'yB{cc}' for cc in range(4)] + [f'dg{i}' for i in range(8)]
            for og in range(2):
                for bri, br in enumerate([2, 3, 0, 1]):
                    wov, wot = wload(WS[onames[br]][l][:, og * 512:(og + 1) * 512], 4, 512)
                    wglv, wglt = wload(win[:, O_G + br * 1024 + og * 512:O_G + br * 1024 + (og + 1) * 512], 8, 512)
                    for oi in range(4):
                        oc = og * 4 + oi
                        by, bgl = brot.next(), brot.next()
                        for cc in range(4):
                            mm(ps[by][:], wov[:, cc, oi * 128:(oi + 1) * 128], srcs[br](cc), cc == 0, cc == 3,
                               [wot, stoks[br](cc)], [PSK[by]])
                        for c in range(8):
                            mm(ps[bgl][:], wglv[:, c, oi * 128:(oi + 1) * 128], uT[:, c, ts0:ts0 + 512], c == 0, c == 7,
                               [wglt, UTK], [PSK[bgl]])
                        si = sgrot.next()
                        act(sg_t[si][:], ps[bgl][:], AF.Sigmoid, [PSK[bgl]], [f'sg{si}'])
                        if bri == 0:
                            tt('dve', fTm[:, oc, :], sg_t[si][:], ps[by][:], ALU.mult, [f'sg{si}', PSK[by]],
                               [f'fm{oc}'] + ALLAB)
                        else:
                            tt('dve', tmpm[:], sg_t[si][:], ps[by][:], ALU.mult, [f'sg{si}', PSK[by]], ['tmpm'])
                            if bri < 3:
                                tt('pool', fTm[:, oc, :], fTm[:, oc, :], tmpm[:], ALU.add, ['tmpm', f'fm{oc}'], [f'fm{oc}'])
                            else:
                                tt('pool', mrg[:, oc, :], fTm[:, oc, :], tmpm[:], ALU.add, ['tmpm', f'fm{oc}'], [f'mrg{oc}'] + ALLAB)
            pendw = []
            for og in range(2):
                wv_, wt_ = wload(WS['w_o'][l][:, og * 512:(og + 1) * 512], 8, 512)
                for oi in range(4):
                    oc = og * 4 + oi
                    b = brot.next()
                    for c in range(8):
                        mm(ps[b][:], wv_[:, c, oi * 128:(oi + 1) * 128], mrg[:, c, :], c == 0, c == 7,
                           [wt_, f'mrg{c}'], [PSK[b]])
                    for (pi_, poc) in pendw:
                        mm(ps[6][:], ones_b, sq_t[pi_][:], poc == 0, poc == 7, [f'sq{pi_}', 'const'], [PSK[6]])
                    pendw = []
                    cp('act', fTm[:, oc, :], ps[b][:], [PSK[b]], [f'fm{oc}'])
                    i = sqrot.next()
                    tt('pool', sq_t[i][:], fTm[:, oc, :], fTm[:, oc, :], ALU.mult, [f'fm{oc}'], [f'sq{i}'])
                    pendw.append((i, oc))
            for (pi_, poc) in pendw:
                mm(ps[6][:], ones_b, sq_t[pi_][:], poc == 0, poc == 7, [f'sq{pi_}', 'const'], [PSK[6]])
            act(lnv_t[:], ps[6][:], AF.Ln, [PSK[6]], ['lnv'], bias=EPS, scale=1.0 / D)
            act(rs_t[1][:], lnv_t[:], AF.Exp, ['lnv'], ['rstd1'], scale=-0.5)
            for oc in range(8):
                stt(fTm[:, oc, :], fTm[:, oc, :], gains[:, l, 3, oc:oc + 1], rs_t[1][:], ALU.mult, ALU.mult,
                    [f'fm{oc}', 'rstd1', 'const'], [f'fm{oc}'])
                tt('pool', hT[:, oc, ts0:ts0 + 512], hT[:, oc, ts0:ts0 + 512], fTm[:, oc, :], ALU.add,
                   [f'fm{oc}', f'hT{oc}_{sub}'], [f'hT{oc}_{sub}'] + ALLAB)

    xrot = Rot(range(2))
    trot = Rot([0, 1, 2, 3])

    def load_seq(s):
        for tb in range(NB):
            i = xrot.next()
            dma(xs_t[i][:], x_d[s, tb * 128:(tb + 1) * 128, :], [], [f'xs{i}'], f'xs{i}')
            for half in range(2):
                b = trot.next()
                for k in range(4):
                    c = half * 4 + k
                    P.add('pe', lambda e, b=b, k=k, c=c, i=i: e.transpose(ps[b][:, k * 128:(k + 1) * 128],
                                                                          xs_t[i][:, c * 128:(c + 1) * 128], ident),
                          [f'xs{i}', 'const'], [PSK[b]])
                cp('dve' if half == 0 else 'act', hT[:, half * 4:half * 4 + 4, tb * 128:(tb + 1) * 128],
                   ps[b][:].rearrange("p (c t) -> p c t", t=128), [PSK[b]], [f'hT{c}_{tb // 4}' for c in range(half * 4, half * 4 + 4)])

    def store_seq(s):
        for tb in range(NB):
            i = xrot.next()
            for half in range(2):
                b = trot.next()
                for k in range(4):
                    c = half * 4 + k
                    P.add('pe', lambda e, b=b, k=k, c=c, tb=tb: e.transpose(ps[b][:, k * 128:(k + 1) * 128],
                                                                            hT[:, c, tb * 128:(tb + 1) * 128], ident),
                          [f'hT{c}_{tb // 4}', 'const'], [PSK[b]])
                cp('dve' if half == 0 else 'act', xs_t[i][:, half * 512:(half + 1) * 512], ps[b][:], [PSK[b]], [f'xs{i}'])
            dma(out_d[s, tb * 128:(tb + 1) * 128, :], xs_t[i][:], [f'xs{i}'], [f'xs{i}'], f'xs{i}')

    convert_next_group()
    barrier()
    for s in range(NSEQ):
        load_seq(s)
        barrier()
        for l in range(NL):
            ffn(l, 1)
            barrier()
            mixer(l)
            barrier()
            ffn(l, 2)
            if l == NL - 1:
                barrier()
        store_seq(s)
        barrier()

    engs = ['pe', 'act', 'dve', 'pool']
    dma_keys = sorted(P.dma_counts.keys())
    sems = {e: es.enter_context(nc.semaphore(f"sem_{e}")) for e in engs}
    dma_sems = {k: es.enter_context(nc.semaphore(f"dsem_{k}")) for k in dma_keys}
    with nc.allow_non_contiguous_dma(reason="tiny parameter-vector loads at setup"), nc.Block() as block:
        P.emit(nc, block, sems, dma_sems, final_waits=['xs0', 'xs1'])
    es.close()
    return nc, len(P.ops)


def _consts(T):
    NB = T // 128
    cf = np.zeros((128, 4, 128), np.float32)
    cf[:, 0, :] = np.eye(128, dtype=np.float32)
    cf[:, 1, :] = 1.0
    kk = np.arange(128)
    cf[:, 2, :] = (kk[:, None] <= kk[None, :]).astype(np.float32)
    E = np.zeros((128, 128), np.float32)
    for h in range(8):
        for j in range(NB):
            for j2 in range(j):
                E[h * NB + j2, h * NB + j] = 1.0
    cf[:, 3, :] = E
    cb = np.zeros((128, 3, 128), np.float32)
    cb[:, 2, :] = np.eye(128, dtype=np.float32)
    cb[:, 0, :] = 1.0
    cb[:, 1, :] = (kk[None, :] >= kk[:, None]).astype(np.float32)
    cb = cb.astype(ml_dtypes.bfloat16)
    al = np.zeros((4, 2, 4, T), np.float32)
    t = np.arange(T, dtype=np.float64)
    for h in range(4):
        slope = 2.0 ** (-2.0 * (h + 1))
        v = 8.0 * slope * t
        hi = v.astype(np.float32).astype(ml_dtypes.bfloat16).astype(np.float64)
        lo = v - hi
        al[h, 0, 0] = -hi
        al[h, 0, 1] = -lo
        al[h, 0, 2:4] = 1.0
        al[h, 1, 0:2] = 1.0
        al[h, 1, 2] = hi
        al[h, 1, 3] = lo
    al = al.astype(ml_dtypes.bfloat16)
    return cf, cb, al


_CACHE = {}


def _get_nc(T, NSEQ, NL):
    key = (T, NSEQ, NL)
    if key not in _CACHE:
        _CACHE[key] = build(T, NSEQ, NL)[0]
    return _CACHE[key]


def run(inputs, n_cores, NL=None, layer0=0):
    x = np.asarray(inputs['x'], np.float32)
    B, T, _ = x.shape
    L = inputs['w_in'].shape[0] if NL is None else NL
    NSEQ = B // n_cores
    nc = _get_nc(T, NSEQ, L)
    cf, cb, al = _consts(T)
    lamc = np.zeros((L, 2), np.float32)
    for l in range(L):
        li = 0.8 - 0.6 * math.exp(-0.3 * (l + layer0))
        lamc[l] = (li, 1.0 - li)
    common = {'cf32': cf, 'cbf': cb, 'alibi': al, 'lamc': lamc}
    for k, v in inputs.items():
        if k == 'x':
            continue
        common[k] = np.ascontiguousarray(np.asarray(v, np.float32)[layer0:layer0 + L])
    in_maps = []
    for i in range(n_cores):
        m = dict(common)
        m['x'] = np.ascontiguousarray(x[i * NSEQ:(i + 1) * NSEQ])
        in_maps.append(m)
    res = run_bass_kernel_spmd(nc, in_maps, core_ids=list(range(n_cores)))
    return np.concatenate([np.asarray(r['out'], np.float32) for r in res.results], axis=0)


def kernel(**inputs):
    return run(inputs, 8)
```
